# Optimizing a Trainium2 kernel written in Bass

```python
import jax, jax.numpy as jnp
from jax import lax
import numpy as np

D_MODEL = 1024
BATCH = 8
SEQ = 2048
DEPTH = 1
DEC_BATCH = 128
DEC_SEQ = 8
PAST_LEN = 16384
PAGE_SIZE = 128

MIX_WIDTH = D_MODEL
A_WIDTH = MIX_WIDTH // 2
A_GROUPS = 4
A_GROUP_DIM = A_WIDTH // A_GROUPS
A_CHUNK = 128
B_WIDTH = MIX_WIDTH - A_WIDTH
B_HEADS = 4
B_DV = B_WIDTH // B_HEADS
B_DK = B_DV // 2
B_LOWRANK = 16
B_GATE_NORMALIZER = 16.0
B_CHUNK = 64
D_FF = 2816
EPS = 1e-6
IN_SIZES = (A_WIDTH, A_WIDTH, B_HEADS * B_DK, B_HEADS * B_DK, B_WIDTH, B_WIDTH, B_LOWRANK)
IN_COLS = sum(IN_SIZES)

kernel_name = 'hymba_chunkmlp_gla_macaron_step'


def _rmsnorm(x, g):
    xf = x.astype(jnp.float32)
    xf = xf * lax.rsqrt(jnp.mean(xf * xf, axis=-1, keepdims=True) + EPS)
    return (xf * g.astype(jnp.float32)).astype(x.dtype)


def _swiglu(x, w_in, w_out):
    gate, up = jnp.split(x @ w_in, 2, axis=-1)
    return (jax.nn.silu(gate) * up) @ w_out


def _chunk_spatial_gate(u, v, w_s, b_s):
    bsz, t, g, c = v.shape
    tp = -(-t // A_CHUNK) * A_CHUNK
    if tp != t:
        v = jnp.pad(v, ((0, 0), (0, tp - t), (0, 0), (0, 0)))
    vc = v.reshape(bsz, tp // A_CHUNK, A_CHUNK, g, c)
    causal = jnp.tril(jnp.ones((A_CHUNK, A_CHUNK), dtype=bool))
    w = jnp.where(causal[None], w_s, jnp.zeros((), w_s.dtype))
    z = jnp.einsum('gts,bnsgc->bntgc', w, vc) + b_s.T[None, None, :, :, None]
    z = z.reshape(bsz, tp, g, c)[:, :t]
    return u * z


def _gla(q, k, v, log_a, s0):
    bsz, t, h, dk = q.shape
    dv = v.shape[-1]
    c = min(B_CHUNK, t)
    tp = -(-t // c) * c
    if tp != t:
        pw = ((0, 0), (0, tp - t), (0, 0), (0, 0))
        q, k, v, log_a = (jnp.pad(a, pw) for a in (q, k, v, log_a))
    n = tp // c
    f32 = jnp.float32
    qf = q.astype(f32).reshape(bsz, n, c, h, dk)
    kf = k.astype(f32).reshape(bsz, n, c, h, dk)
    vf = v.astype(f32).reshape(bsz, n, c, h, dv)
    b = jnp.cumsum(log_a.astype(f32).reshape(bsz, n, c, h, dk), axis=2)
    b_ref = b[:, :, c // 2][:, :, None]
    b_last = b[:, :, c - 1]
    causal = jnp.tril(jnp.ones((c, c), dtype=bool))
    scores = jnp.einsum('bnthd,bnshd->bnhts', qf * jnp.exp(b - b_ref), kf * jnp.exp(b_ref - b))
    scores = jnp.where(causal, scores, 0.0)
    o_intra = jnp.einsum('bnhts,bnshv->bnthv', scores, vf)
    chunk_upd = jnp.einsum('bnshd,bnshv->bnhdv', kf * jnp.exp(b_last[:, :, None] - b), vf)
    chunk_decay = jnp.exp(b_last)

    def step(s, xs):
        dec, upd = xs
        return dec[..., None] * s + upd, s

    s_final, s_prev = lax.scan(step, s0.astype(f32),
                               (jnp.moveaxis(chunk_decay, 1, 0), jnp.moveaxis(chunk_upd, 1, 0)))
    s_prev = jnp.moveaxis(s_prev, 0, 1)
    o_inter = jnp.einsum('bnthd,bnhdv->bnthv', qf * jnp.exp(b), s_prev)
    o = (o_intra + o_inter).reshape(bsz, tp, h, dv)[:, :t]
    return o, s_final


def _mixer(h, s0, w_in, a_ws, a_bs, a_vnorm, a_onorm, b_wa2, b_ba, b_onorm, w_out):
    bsz, t, _ = h.shape
    p = h @ w_in
    pu, pv, pq, pk, pvb, pr, plr = jnp.split(p, np.cumsum(IN_SIZES)[:-1].tolist(), axis=-1)
    u = jax.nn.gelu(pu).reshape(bsz, t, A_GROUPS, A_GROUP_DIM)
    va = _rmsnorm(jax.nn.gelu(pv).reshape(bsz, t, A_GROUPS, A_GROUP_DIM), a_vnorm)
    ya = _chunk_spatial_gate(u, va, a_ws, a_bs).reshape(bsz, t, A_WIDTH)
    ya = _rmsnorm(ya, a_onorm)
    q = pq.reshape(bsz, t, B_HEADS, B_DK) * (B_DK ** -0.5)
    k = pk.reshape(bsz, t, B_HEADS, B_DK)
    vb = pvb.reshape(bsz, t, B_HEADS, B_DV)
    log_a = jax.nn.log_sigmoid((plr @ b_wa2 + b_ba).astype(jnp.float32)) / B_GATE_NORMALIZER
    o, s_new = _gla(q, k, vb, log_a.reshape(bsz, t, B_HEADS, B_DK), s0)
    yb = _rmsnorm(o, b_onorm) * jax.nn.silu(pr).reshape(bsz, t, B_HEADS, B_DV)
    y = jnp.concatenate([ya, yb.reshape(bsz, t, B_WIDTH).astype(ya.dtype)], axis=-1) @ w_out
    return y, s_new, va.reshape(bsz, t, A_WIDTH)


def _layer(x, s0, ffn1_norm, ffn1_w_in, ffn1_w_out, mix_norm, w_in, a_ws, a_bs, a_vnorm,
           a_onorm, b_wa2, b_ba, b_onorm, w_out, ffn2_norm, ffn2_w_in, ffn2_w_out):
    x = x + 0.5 * _swiglu(_rmsnorm(x, ffn1_norm), ffn1_w_in, ffn1_w_out)
    y, s_new, va = _mixer(_rmsnorm(x, mix_norm), s0, w_in, a_ws, a_bs, a_vnorm, a_onorm,
                          b_wa2, b_ba, b_onorm, w_out)
    x = x + y
    x = x + 0.5 * _swiglu(_rmsnorm(x, ffn2_norm), ffn2_w_in, ffn2_w_out)
    return x, s_new, va


def setup_inputs(seed: int = 0) -> dict:
    key = jax.random.key(seed)
    ks = jax.random.split(key, 24)
    nrm = lambda k, shape, s: jax.random.normal(k, shape, jnp.float32) * s
    gain = lambda k, shape: 1.0 + 0.01 * jax.random.normal(k, shape, jnp.float32)
    L = DEPTH
    return {
        'x_prompt': nrm(ks[0], (BATCH, SEQ, D_MODEL), 1.0),
        'x_sample': nrm(ks[1], (DEC_BATCH, DEC_SEQ, D_MODEL), 1.0),
        'state_gla': nrm(ks[2], (L, DEC_BATCH, B_HEADS, B_DK, B_DV), 1.0),
        'ffn1_norm': gain(ks[3], (L, D_MODEL)),
        'ffn1_w_in': nrm(ks[4], (L, D_MODEL, 2 * D_FF), D_MODEL ** -0.5),
        'ffn1_w_out': nrm(ks[5], (L, D_FF, D_MODEL), D_FF ** -0.5),
        'mix_norm': gain(ks[6], (L, D_MODEL)),
        'w_in': nrm(ks[7], (L, D_MODEL, IN_COLS), D_MODEL ** -0.5),
        'a_ws': nrm(ks[8], (L, A_GROUPS, A_CHUNK, A_CHUNK), A_CHUNK ** -0.5),
        'a_bs': 1.0 + 0.1 * jax.random.normal(ks[9], (L, A_GROUPS, A_CHUNK), jnp.float32),
        'a_vnorm': gain(ks[10], (L, A_GROUPS, A_GROUP_DIM)),
        'a_onorm': gain(ks[11], (L, A_WIDTH)),
        'b_wa2': nrm(ks[12], (L, B_LOWRANK, B_HEADS * B_DK), B_LOWRANK ** -0.5),
        'b_ba': nrm(ks[13], (L, B_HEADS * B_DK), 0.1),
        'b_onorm': gain(ks[14], (L, B_HEADS, B_DV)),
        'w_out': nrm(ks[15], (L, MIX_WIDTH, D_MODEL), MIX_WIDTH ** -0.5),
        'ffn2_norm': gain(ks[16], (L, D_MODEL)),
        'ffn2_w_in': nrm(ks[17], (L, D_MODEL, 2 * D_FF), D_MODEL ** -0.5),
        'ffn2_w_out': nrm(ks[18], (L, D_FF, D_MODEL), D_FF ** -0.5),
        'final_norm': gain(ks[19], (D_MODEL,)),
    }


def reference(x_prompt, x_sample, state_gla, ffn1_norm, ffn1_w_in, ffn1_w_out, mix_norm, w_in,
              a_ws, a_bs, a_vnorm, a_onorm, b_wa2, b_ba, b_onorm, w_out, ffn2_norm, ffn2_w_in,
              ffn2_w_out, final_norm):
    yp, ys = x_prompt, x_sample
    s_prompt, s_sample, v_sample = [], [], []
    for i in range(DEPTH):
        w = (ffn1_norm[i], ffn1_w_in[i], ffn1_w_out[i], mix_norm[i], w_in[i], a_ws[i], a_bs[i],
             a_vnorm[i], a_onorm[i], b_wa2[i], b_ba[i], b_onorm[i], w_out[i], ffn2_norm[i],
             ffn2_w_in[i], ffn2_w_out[i])
        s0_prompt = jnp.zeros((x_prompt.shape[0], B_HEADS, B_DK, B_DV), jnp.float32)
        yp, sp, _ = _layer(yp, s0_prompt, *w)
        ys, ss, vs = _layer(ys, state_gla[i], *w)
        s_prompt.append(sp)
        s_sample.append(ss)
        v_sample.append(vs)
    y_prompt = _rmsnorm(yp, final_norm)
    y_sample = _rmsnorm(ys, final_norm)
    return (y_prompt, y_sample, jnp.stack(s_prompt), jnp.stack(s_sample), jnp.stack(v_sample))
```

```python
import contextlib
import numpy as np
import concourse.bass as bass
import concourse.mybir as mybir
from concourse.bass_utils import run_bass_kernel_spmd

F32 = mybir.dt.float32
BF16 = mybir.dt.bfloat16
AF = mybir.ActivationFunctionType
ALU = mybir.AluOpType
AX = mybir.AxisListType

NCORES = 8
NT = 17
D = 1024
DFF = 2816
NJ = DFF // 128
INC = 2576
EPS = 1e-6
GS = 4
RSTD_POOL = True
ENGS = ["pe", "act", "dve", "pool", "sp"]
NCHAN = 8


class Buf:
    __slots__ = ("name", "lw", "readers")

    def __init__(self, name=""):
        self.name = name
        self.lw = None
        self.readers = {}


class Prog:
    def __init__(self, nc):
        self.nc = nc
        self.ops = {e: [] for e in ENGS}
        self.dma_count = {}
        self.dma_rr = {e: 0 for e in ENGS}
        self.out_tokens = []
        self.bar = set()

    @staticmethod
    def _rkey(t):
        return (t[0], t[1]) if t[0] == "c" else (t[0], t[1], t[2])

    def _track(self, r, w, token):
        raw, other = set(), set()
        for b in r:
            if b.lw is not None:
                raw.add(b.lw)
        for b in w:
            if b.lw is not None:
                if b.lw[0] == "c" and b.lw[1] != "pe":
                    raw.add(b.lw)
                else:
                    other.add(b.lw)
            for t in b.readers.values():
                if t[0] == "c" and t[1] != "pe":
                    raw.add(t)
                else:
                    other.add(t)
        k = self._rkey(token)
        for b in r:
            b.readers[k] = token
        for b in w:
            b.readers = {}
            b.lw = token
        raw.discard(token)
        other.discard(token)
        return raw, other

    def barrier(self):
        self.bar = self.snapshot()

    def snapshot(self):
        bar = set()
        for e in ENGS:
            if self.ops[e]:
                for i in range(len(self.ops[e]) - 1, -1, -1):
                    if self.ops[e][i]["dma"] is None:
                        bar.add(("c", e, i))
                        break
        for (q, c), n in self.dma_count.items():
            bar.add(("d", q, c, 16 * n))
        return bar

    def op(self, eng, fn, r=(), w=(), deps=()):
        idx = len(self.ops[eng])
        token = ("c", eng, idx)
        raw, other = self._track(r, w, token)
        d = set(deps)
        for t in raw:
            if t[0] == "c" and t[1] == eng and eng == "pe":
                continue
            d.add(t)
        for t in other | self.bar:
            if t[0] == "c" and t[1] == eng:
                continue
            d.add(t)
        self.ops[eng].append(dict(fn=fn, deps=d, dma=None))
        return token

    def dma(self, queue, fn, r=(), w=(), deps=(), is_output=False):
        chan = self.dma_rr[queue] % NCHAN
        self.dma_rr[queue] += 1
        key = (queue, chan)
        n = self.dma_count.get(key, 0)
        self.dma_count[key] = n + 1
        token = ("d", queue, chan, 16 * (n + 1))
        raw, other = self._track(r, w, token)
        d = set(deps) | raw | other | self.bar
        d.discard(token)
        if n > 0:
            d.add(("d", queue, chan, 16 * n))
        self.ops[queue].append(dict(fn=fn, deps=d, dma=(chan, 16 * (n + 1))))
        if is_output:
            self.out_tokens.append(token)
        return token

    def emit(self, sems, dma_sems, final_engine="sp"):
        if self.out_tokens:
            self.ops[final_engine].append(dict(fn=None, deps=set(self.out_tokens), dma=None))
        need = {e: set() for e in ENGS}
        for e in ENGS:
            for o in self.ops[e]:
                for t in o["deps"]:
                    if t[0] == "c":
                        need[t[1]].add(t[2])
        val = {e: {} for e in ENGS}
        for e in ENGS:
            c = 0
            for i in range(len(self.ops[e])):
                if i in need[e]:
                    c += 1
                    val[e][i] = c
        stats = {e: [len(self.ops[e]), 0, len(need[e])] for e in ENGS}

        def run(e, eng):
            seen = {}
            for i, o in enumerate(self.ops[e]):
                waits = {}
                for t in o["deps"]:
                    if t[0] == "c":
                        key = ("c", t[1]); v = val[t[1]][t[2]]; s = sems[t[1]]
                    else:
                        key = ("d", t[1], t[2]); v = t[3]; s = dma_sems[(t[1], t[2])]
                    if v > seen.get(key, 0):
                        if key not in waits or waits[key][1] < v:
                            waits[key] = (s, v)
                for key, (s, v) in waits.items():
                    eng.wait_ge(s, v)
                    seen[key] = v
                    stats[e][1] += 1
                if o["fn"] is None:
                    continue
                ins = o["fn"](eng)
                if o["dma"] is not None:
                    chan, v = o["dma"]
                    ins.then_inc(dma_sems[(e, chan)], 16)
                elif i in need[e]:
                    ins.then_inc(sems[e], 1)
        return run, stats


def bc(ap, axis, n):
    lst = [list(x) for x in ap.ap]
    lst.insert(axis, [0, n])
    return bass.AP(ap.tensor, ap.offset, lst)


def build(stage=3):
    nc = bass.Bass("TRN2", target_bir_lowering=False)
    dt_in = lambda name, shape: nc.dram_tensor(name, list(shape), F32, kind="ExternalInput").ap()
    dt_out = lambda name, shape: nc.dram_tensor(name, list(shape), F32, kind="ExternalOutput").ap()
    xin = dt_in("xin", [NT, 128, D])
    state = dt_in("state", [16, 4, 64, 128])
    f1n = dt_in("f1n", [D]); f1wi = dt_in("f1wi", [D, 2 * DFF]); f1wo = dt_in("f1wo", [DFF, D])
    mxn = dt_in("mxn", [D]); wi = dt_in("wi", [D, INC])
    aws = dt_in("aws", [4, 128, 128]); abs_ = dt_in("abs", [4, 128]); avn = dt_in("avn", [512])
    aon = dt_in("aon", [512]); wa2 = dt_in("wa2", [16, 256]); bba = dt_in("bba", [256])
    bon = dt_in("bon", [512]); wo = dt_in("wo", [D, D])
    f2n = dt_in("f2n", [D]); f2wi = dt_in("f2wi", [D, 2 * DFF]); f2wo = dt_in("f2wo", [DFF, D])
    fnn = dt_in("fnn", [D])
    yout = dt_out("yout", [NT, 128, D])
    sp_out = dt_out("sp_out", [4, 64, 128])
    ss_out = dt_out("ss_out", [16, 4, 64, 128])
    cv_out = dt_out("cv_out", [128, 512])

    with contextlib.ExitStack() as st:
        E = st.enter_context
        sb = lambda name, shape, dt: E(nc.sbuf_tensor(name, list(shape), dt))
        X = sb("X", [128, NT, D], F32)
        ARENA_BYTES = 115 * 1024
        arena = sb("arena", [128, ARENA_BYTES // 2], BF16)
        ps = E(nc.psum_tensor("ps", [128, 4096], F32))
        psb = ps[:].bitcast(BF16)
        ident_b = sb("ident_b", [128, 128], BF16)
        ident_f = sb("ident_f", [128, 128], F32)
        U = sb("U", [128, 128], F32); SL = sb("SL", [128, 128], F32)
        Us = sb("Us", [128, 128], F32); SLs = sb("SLs", [128, 128], F32)
        blkT = sb("blkT", [128, 16], F32)
        Ub = sb("Ub", [128, 128], BF16); SLb = sb("SLb", [128, 128], BF16)
        Usb = sb("Usb", [128, 128], BF16); SLsb = sb("SLsb", [128, 128], BF16)
        wa2hi = sb("wa2hi", [17, 256], BF16); wa2lo = sb("wa2lo", [17, 256], BF16)
        plrhi = sb("plrhi", [17, 128], BF16); plrlo = sb("plrlo", [17, 128], BF16)
        WsT = sb("WsT", [128, 4, 128], BF16); WsTs = sb("WsTs", [128, 4, 128], BF16)
        Wst = sb("Wst", [128, 4, 128], F32)
        Wst2 = sb("Wst2", [128, 4, 128], F32)
        bcol = sb("bcol", [128, 4], F32); bcols = sb("bcols", [128, 4], F32)
        gbc = sb("gbc", [128, D], F32)
        gv_bc = sb("gv_bc", [128, 512], F32); ga_bc = sb("ga_bc", [128, 512], F32); gb_bc = sb("gb_bc", [128, 512], F32)
        wa2b = sb("wa2b", [17, 256], F32)
        plrT = sb("plrT", [17, 128], F32)
        ssq = sb("ssq", [128, 4 * NT], F32)
        rsd = sb("rsd", [128, 4 * NT], F32)
        sm = sb("sm", [128, 64], F32)

        sems = {e: E(nc.semaphore("s_" + e)) for e in ENGS}
        dsems = {(q, c): E(nc.semaphore(f"d_{q}_{c}")) for q in ("sp", "pool") for c in range(NCHAN)}
        P = Prog(nc)

        def view(off, shape, dt):
            n = int(np.prod(shape))
            isz = 4 if dt == F32 else 2
            assert off % 4 == 0
            a = off // 2
            b = a + n * isz // 2
            assert b * 2 <= ARENA_BYTES, (off, shape)
            ap = arena[:, a:b]
            if dt == F32:
                ap = ap.bitcast(F32)
            if len(shape) == 2:
                ap = ap.rearrange("p (a b) -> p a b", a=shape[0])
            elif len(shape) == 3:
                ap = ap.rearrange("p (a b c) -> p a b c", a=shape[0], b=shape[1])
            return ap

        class Alloc:
            def __init__(self):
                self.off = 0

            def __call__(self, shape, dt):
                n = int(np.prod(shape)) * (4 if dt == F32 else 2)
                n = (n + 31) // 32 * 32
                v = view(self.off, shape, dt)
                self.off += n
                return v

        bank = lambda b, n=512: ps[:, b * 512:b * 512 + n]
        bankb = lambda b: psb[:, b * 1024:(b + 1) * 1024]
        Bbank = [Buf(f"bank{i}") for i in range(8)]
        BX = [Buf(f"X{t}") for t in range(NT)]

        def ACT(out, in_, func, r, w, **kw):
            return P.op("act", lambda e: e.activation(out=out, in_=in_, func=func, **kw), r=r, w=w)

        def TT(eng, out, in0, in1, op, r, w):
            return P.op(eng, lambda e: e.tensor_tensor(out=out, in0=in0, in1=in1, op=op), r=r, w=w)

        def STT(out, in0, scalar, in1, op0, op1, r, w):
            return P.op("dve", lambda e: e.scalar_tensor_tensor(out=out, in0=in0, scalar=scalar, in1=in1, op0=op0, op1=op1), r=r, w=w)

        def TS(eng, out, in0, s1, op0, r, w, s2=None, op1=None):
            if op1 is None:
                return P.op(eng, lambda e: e.tensor_scalar(out=out, in0=in0, scalar1=s1, scalar2=None, op0=op0), r=r, w=w)
            return P.op(eng, lambda e: e.tensor_scalar(out=out, in0=in0, scalar1=s1, scalar2=s2, op0=op0, op1=op1), r=r, w=w)

        def CP(eng, out, in_, r, w):
            return P.op(eng, lambda e: e.tensor_copy(out=out, in_=in_), r=r, w=w)

        def MM(out, lhsT, rhs, start, stop, r, w):
            return P.op("pe", lambda e: e.matmul(out, lhsT=lhsT, rhs=rhs, start=start, stop=stop), r=r, w=w)

        def TR(out, in_, ident, r, w):
            return P.op("pe", lambda e: e.transpose(out=out, in_=in_, identity=ident), r=r, w=w)

        def DMA(q, out, in_, r, w, is_output=False, slow=False, deps=()):
            if slow:
                return P.dma(q, lambda e: e.dma_start(out=out, in_=in_, allow_slow_non_contiguous=True), r=r, w=w, is_output=is_output, deps=deps)
            return P.dma(q, lambda e: e.dma_start(out=out, in_=in_), r=r, w=w, is_output=is_output, deps=deps)

        def rstd_from_ss(ss_ap, rs_ap, n, bss, brs, pool=True):
            if RSTD_POOL and pool:
                wdt = rs_ap.shape[1]
                P.op("pool", lambda e: e.tensor_scalar(out=rs_ap, in0=ss_ap, scalar1=1.0 / n, scalar2=EPS, op0=ALU.mult, op1=ALU.add),
                     r=[bss], w=[brs])
                P.op("pool", lambda e: e.tensor_tensor(out=rs_ap, in0=rs_ap, in1=neghalf[:, 0:wdt], op=ALU.pow), r=[brs, Bc], w=[brs])
            else:
                ACT(rs_ap, ss_ap, AF.Sqrt, r=[bss, Bc], w=[brs], scale=1.0 / n, bias=EPS_AP)
                P.op("dve", lambda e: e.reciprocal(out=rs_ap, in_=rs_ap), r=[brs], w=[brs])

        Bc = Buf("consts")
        epsc = sb("epsc", [128, 1], F32)
        onec = sb("onec", [128, 1], F32)
        neghalf = sb("neghalf", [128, 4], F32)
        EPS_AP = epsc[:, 0:1]
        ONE_AP = onec[:, 0:1]
        Bident = Buf("ident"); BU = Buf("U"); BWs = Buf("Ws"); Bblk = Buf("blk")
        P.op("pool", lambda e: e.memset(epsc[:], EPS), w=[Bc])
        P.op("pool", lambda e: e.memset(onec[:], 1.0), w=[Bc])
        P.op("pool", lambda e: e.memset(neghalf[:], -0.5), w=[Bc])
        ACT(sm[:, 25:26], ONE_AP, AF.Square, r=[Bc], w=[Buf("warm0")])
        P.op("pool", lambda e: e.memset(ident_f[:], 1.0), w=[Bident])
        P.op("pool", lambda e: e.affine_select(out=ident_f[:], in_=ident_f[:], pattern=[[-1, 128]], compare_op=ALU.is_equal,
                                               fill=0.0, base=0, channel_multiplier=1), r=[Bident], w=[Bident])
        CP("pool", ident_b[:], ident_f[:], r=[Bident], w=[Bident])
        P.op("pool", lambda e: e.memset(U[:], 1.0), w=[BU])
        P.op("pool", lambda e: e.affine_select(out=U[:], in_=U[:], pattern=[[1, 128]], compare_op=ALU.is_ge,
                                               fill=0.0, base=0, channel_multiplier=-1), r=[BU], w=[BU])
        P.op("pool", lambda e: e.memset(SL[:], 1.0), w=[BU])
        P.op("pool", lambda e: e.affine_select(out=SL[:], in_=SL[:], pattern=[[-1, 128]], compare_op=ALU.is_ge,
                                               fill=0.0, base=-1, channel_multiplier=1), r=[BU], w=[BU])
        P.op("pool", lambda e: e.memset(blkT[:], 1.0), w=[Bblk])
        P.op("pool", lambda e: e.affine_select(out=blkT[:], in_=blkT[:], pattern=[[-8, 16]], compare_op=ALU.is_ge,
                                               fill=0.0, base=0, channel_multiplier=1), r=[Bblk], w=[Bblk])
        P.op("pool", lambda e: e.affine_select(out=blkT[:], in_=blkT[:], pattern=[[8, 16]], compare_op=ALU.is_ge,
                                               fill=0.0, base=7, channel_multiplier=-1), r=[Bblk], w=[Bblk])
        blk_v = bc(blkT[:], 2, 8)
        TT("pool", Us[:].rearrange("p (a b) -> p a b", b=8), U[:].rearrange("p (a b) -> p a b", b=8), blk_v, ALU.mult, r=[BU, Bblk], w=[BU])
        TT("pool", SLs[:].rearrange("p (a b) -> p a b", b=8), SL[:].rearrange("p (a b) -> p a b", b=8), blk_v, ALU.mult, r=[BU, Bblk], w=[BU])
        for dst_, src_ in ((Ub, U), (SLb, SL), (Usb, Us), (SLsb, SLs)):
            CP("pool", dst_[:], src_[:], r=[BU], w=[BU])
        for t in range(4):
            DMA("sp", X[:, t, :], xin[t], r=[], w=[BX[t]])

        def load_rest_x(deps):
            tok = None
            for t in range(4, NT):
                tok = DMA("sp", X[:, t, :], xin[t], r=[], w=[BX[t]], deps=deps)
            return tok

        tok0 = P.op("pool", lambda e: e.memset(Wst2[:], 0.0), w=[Buf()])
        def load_consts():
          DMA("sp", Wst[:], aws.rearrange("g t s -> t g s"), r=[], w=[Buf()])
          for i in range(16):
            DMA("sp", Wst2[8 * i:8 * i + 8, :, 8 * i:8 * i + 8], aws[:, 0:8, 0:8].rearrange("g t s -> t g s"), r=[], w=[Buf()], deps=[tok0])
          DMA("sp", bcol[:], abs_.rearrange("g t -> t g"), r=[], w=[Buf()], slow=True)
          for i in range(16):
            DMA("sp", bcols[8 * i:8 * i + 8, :], abs_[:, 0:8].rearrange("g t -> t g"), r=[], w=[Buf()], slow=True)
          DMA("sp", gv_bc[:], avn.partition_broadcast(128), r=[], w=[Buf()])
          DMA("sp", ga_bc[:], aon.partition_broadcast(128), r=[], w=[Buf()])
          DMA("sp", gb_bc[:], bon.partition_broadcast(128), r=[], w=[Buf()])
          DMA("sp", wa2b[0:16, :], wa2, r=[], w=[Buf()])
          DMA("sp", wa2b[16:17, :], bba.rearrange("(o n) -> o n", o=1), r=[], w=[Buf()])

        BplrT = Buf("plrT"); Bwa2 = Buf("wa2")
        P.op("pool", lambda e: e.memset(plrT[:], 1.0), w=[BplrT])

        def setup_ws():
            for g in range(4):
                TR(bank(0)[:, g * 128:(g + 1) * 128], Wst[:, g, :], ident_f[:], r=[Bident], w=[Bbank[0]])
            TT("dve", WsT[:], bank(0).rearrange("p (g t) -> p g t", g=4), bc(U[:], 1, 4), ALU.mult, r=[Bbank[0], BU], w=[BWs])
            for g in range(4):
                TR(bank(1)[:, g * 128:(g + 1) * 128], Wst2[:, g, :], ident_f[:], r=[Bident], w=[Bbank[1]])
            TT("dve", WsTs[:], bank(1).rearrange("p (g t) -> p g t", g=4), bc(Us[:], 1, 4), ALU.mult, r=[Bbank[1], BU], w=[BWs])
            CP("dve", wa2hi[:], wa2b[:], r=[], w=[Bwa2])
            TT("dve", wa2lo[:], wa2b[:], wa2hi[:], ALU.subtract, r=[Bwa2], w=[Bwa2])

        Bgbc = Buf("gbc")

        pre_brs = {}

        def norm_stats(ph, t, xnb_ap, Bxnb, pool=None):
            col = ph * NT + t
            bss = Buf(); brs = Buf()
            ACT(xnb_ap, X[:, t, :], AF.Square, r=[BX[t]], w=[Bxnb, bss], accum_out=ssq[:, col:col + 1])
            rstd_from_ss(ssq[:, col:col + 1], rsd[:, col:col + 1], D, bss, brs, pool=(ph == 2) if pool is None else pool)
            return brs

        def norm_scale(ph, t, xnb_ap, Bxnb, brs):
            col = ph * NT + t
            STT(xnb_ap, X[:, t, :], rsd[:, col:col + 1], gbc[:], ALU.mult, ALU.mult, r=[BX[t], brs, Bgbc], w=[Bxnb])

        def norm_pre(ph, t, xnb_ap, Bxnb, sqj_ap=None, Bsqj=None):
            brs = pre_brs.get((ph, t))
            if brs is None:
                brs = norm_stats(ph, t, xnb_ap, Bxnb)
            norm_scale(ph, t, xnb_ap, Bxnb, brs)

        def norm_T(ph, t, xnb_ap, Bxnb, sqj_ap, Bsqj, dstT, BdstT, pbank, pre=True, cp_act=False):
            if pre:
                norm_pre(ph, t, xnb_ap, Bxnb, sqj_ap, Bsqj)
            for k in range(8):
                TR(bankb(pbank)[:, k * 128:(k + 1) * 128], xnb_ap[:, k * 128:(k + 1) * 128], ident_b[:], r=[Bxnb, Bident], w=[Bbank[pbank]])
            if not cp_act:
                CP("dve", dstT, bankb(pbank).rearrange("p (k n) -> p k n", k=8), r=[Bbank[pbank]], w=[BdstT])
            else:
                ACT(dstT, bankb(pbank).rearrange("p (k n) -> p k n", k=8), AF.Copy, r=[Bbank[pbank]], w=[BdstT])

        FA = Alloc()
        XT = FA([8, NT * 128], BF16)
        SLOT0_OFF = FA.off
        WG = [None, None]; WU = [None, None]; WO = [None, None]
        for s_ in range(2):
            WG[s_] = FA([8, GS * 128], BF16); WU[s_] = FA([8, GS * 128], BF16); WO[s_] = FA([GS, D], BF16)
        SLOT1_OFF = SLOT0_OFF + 3 * 8 * GS * 128 * 2
        hT = [FA([GS, 512], BF16) for _ in range(2)]
        sil = [FA([512], F32) for _ in range(2)]
        xnbF = [FA([D], BF16) for _ in range(2)]
        yfin = [FA([D], F32) for _ in range(2)]
        BWg = [Buf("Wg0"), Buf("Wg1")]; BWu = [Buf("Wu0"), Buf("Wu1")]; BWo = [Buf("Wo0"), Buf("Wo1")]
        groups = [[0, 1, 2, 3], [4, 5, 6, 7], [8, 9, 10, 11], [12, 13, 14, 15], [16, 17], [18, 19, 20, 21]]
        WINa = view(SLOT0_OFF, [8, 1536], BF16)
        WINb = view(0, [8, INC - 1536], BF16)
        WOUT = view(8 * (INC - 1536) * 2, [8, D], BF16)
        assert 8 * (INC - 1536) * 2 + 8 * D * 2 <= SLOT0_OFF and SLOT0_OFF + 8 * 1536 * 2 == SLOT1_OFF
        BWINc = [Buf(f"WIN{i}") for i in range(5)]; BWOUT = Buf("WOUT"); BWINlr = Buf("WINlr")

        last_load = []

        def load_group(w_in_d, w_out_d, gi, deps=(), extra_w=(), wo_after=False):
            js = groups[gi]; n = len(js); s = gi % 2; j0 = js[0]
            ex = list(extra_w)
            tkg = DMA("pool", WG[s][:, :, 0:n * 128], w_in_d[:, j0 * 128:(j0 + n) * 128].rearrange("(k p) n -> p k n", p=128), r=[], w=[BWg[s]] + ex, deps=deps)
            tku = DMA("pool", WU[s][:, :, 0:n * 128], w_in_d[:, DFF + j0 * 128:DFF + (j0 + n) * 128].rearrange("(k p) n -> p k n", p=128), r=[], w=[BWu[s]] + ex)
            tko = DMA("pool", WO[s][:, 0:n, :], w_out_d[j0 * 128:(j0 + n) * 128, :].rearrange("(j p) n -> p j n", p=128), r=[], w=[BWo[s]] + ex,
                      deps=([tku] if wo_after else []))
            last_load[:] = [tkg, tku, tko]
            return tku

        def prefetch_wina():
            for ci, (c0, c1) in enumerate([(0, 512), (512, 1024), (1024, 1536)]):
                DMA("pool", WINa[:, :, c0:c1], wi[:, c0:c1].rearrange("(k p) n -> p k n", p=128), r=[], w=[BWg[0], BWu[0], BWo[0], BWINc[ci]])

        def ffn(ph, gam, w_in_d, w_out_d, final=False, barrier=True, preloaded0=False, after_slot0=None, tail=None, xnb_override=None,
                gbc_preloaded=False, next_gamma=None):
            if barrier:
                P.barrier()
                if preloaded0:
                    P.bar -= set(mix_export.get("g0_tokens", []))
            xnb = xnb_override if xnb_override is not None else xnbF
            tail = list(tail) if tail else []
            defer_g1 = bool(tail)
            BXT = [Buf(f"XT{t}") for t in range(NT)]
            BhT = [Buf(), Buf()]; Bsil = [Buf(), Buf()]; Bxnb = [Buf(), Buf()]; Byfin = [Buf(), Buf()]

            if not gbc_preloaded:
                DMA("sp", gbc[:], gam.partition_broadcast(128), r=[], w=[Bgbc])
            tokw = None
            if not preloaded0:
                tokw = load_group(w_in_d, w_out_d, 0, wo_after=True)
            if ph == 0:
                tokx = load_rest_x([tokw])
                load_consts()
                load_group(w_in_d, w_out_d, 1, deps=[tokx])
            elif not defer_g1:
                load_group(w_in_d, w_out_d, 1)
            blocks = [list(range(b * 4, b * 4 + 4)) for b in range(4)] + [[16]]
            tstate = {"post": None, "g1": not defer_g1}
            cnt = {"gu": 0, "y": 0, "hb": 0}
            brs0 = {}
            for t in blocks[0]:
                if (ph, t) in pre_brs:
                    brs0[t] = (None, pre_brs[(ph, t)])
                    continue
                col = ph * NT + t
                bss = Buf(); brs0[t] = Buf()
                ACT(xnb[t % 2], X[:, t, :], AF.Square, r=[BX[t]], w=[Bxnb[t % 2], bss], accum_out=ssq[:, col:col + 1])
                brs0[t] = (bss, brs0[t])
            for t in blocks[0]:
                if brs0[t][0] is None:
                    continue
                col = ph * NT + t
                rstd_from_ss(ssq[:, col:col + 1], rsd[:, col:col + 1], D, brs0[t][0], brs0[t][1], pool=False)
            b0 = blocks[0]
            norm_scale(ph, b0[0], xnb[b0[0] % 2], Bxnb[b0[0] % 2], brs0[b0[0]][1])
            for i, t in enumerate(b0):
                if i + 1 < len(b0):
                    t1 = b0[i + 1]
                    norm_scale(ph, t1, xnb[t1 % 2], Bxnb[t1 % 2], brs0[t1][1])
                norm_T(ph, t, xnb[t % 2], Bxnb[t % 2], None, None, XT[:, :, t * 128:(t + 1) * 128], BXT[t], 4 + t % 4, pre=False, cp_act=True)
            pending = [t for b in blocks[1:] for t in b]

            def GU(gi, bi):
                js = groups[gi]; s = gi % 2; tiles = blocks[bi]; ntok = len(tiles) * 128; t0 = tiles[0] * 128
                hb = cnt["hb"] % 2; cnt["hb"] += 1
                for jj, j in enumerate(js):
                    pg = cnt["gu"] % 2; cnt["gu"] += 1
                    rx = [BXT[t] for t in tiles]
                    tn = pending.pop(0) if (gi == 0 and pending) else None
                    if tn is not None:
                        norm_pre(ph, tn, xnb[tn % 2], Bxnb[tn % 2])
                    tpost = None
                    if gi == 0 and tail:
                        tpre, tpost = tail.pop(0)
                        tpre()
                    for k in range(8):
                        MM(bank(pg)[:, 0:ntok], WG[s][:, k, jj * 128:(jj + 1) * 128], XT[:, k, t0:t0 + ntok], k == 0, k == 7,
                           r=rx + [BWg[s]], w=[Bbank[pg]])
                    for k in range(8):
                        MM(bank(2 + pg)[:, 0:ntok], WU[s][:, k, jj * 128:(jj + 1) * 128], XT[:, k, t0:t0 + ntok], k == 0, k == 7,
                           r=rx + [BWu[s]], w=[Bbank[2 + pg]])
                    ACT(sil[pg][:, 0:ntok], bank(pg)[:, 0:ntok], AF.Silu, r=[Bbank[pg]], w=[Bsil[pg]])
                    TT("dve", hT[hb][:, jj, 0:ntok], sil[pg][:, 0:ntok], bank(2 + pg)[:, 0:ntok], ALU.mult,
                       r=[Bsil[pg], Bbank[2 + pg]], w=[BhT[hb]])
                    if tpost is not None:
                        tpost()
                    if gi == 0 and not tail and not tstate["g1"]:
                        tstate["g1"] = True
                        load_group(w_in_d, w_out_d, 1, deps=list(P.snapshot()))
                    if tn is not None:
                        norm_T(ph, tn, xnb[tn % 2], Bxnb[tn % 2], None, None, XT[:, :, tn * 128:(tn + 1) * 128], BXT[tn], 4 + tn % 4, pre=False)
                        if final and not pending:
                            DMA("sp", gbc[:], fnn.partition_broadcast(128), r=[], w=[Bgbc])
                        if next_gamma is not None and not pending and not tstate.get("ng"):
                            tstate["ng"] = True
                            DMA("sp", gbc[:], next_gamma.partition_broadcast(128), r=[], w=[Bgbc])
                return hb

            fin_pending = []

            def fin_emit():
                t, col, brs = fin_pending.pop(0)
                yb = t % 2
                STT(yfin[yb], X[:, t, :], rsd[:, col:col + 1], gbc[:], ALU.mult, ALU.mult, r=[BX[t], brs, Bgbc], w=[Byfin[yb]])
                DMA("sp", yout[t], yfin[yb], r=[Byfin[yb]], w=[], is_output=True)

            def Y(gi, bi, hb):
                js = groups[gi]; s = gi % 2; tiles = blocks[bi]
                last = (gi == len(groups) - 1)
                for ti, t in enumerate(tiles):
                    py = cnt["y"] % 2; cnt["y"] += 1
                    b0 = 4 + 2 * py
                    for jj in range(len(js)):
                        for n in range(2):
                            MM(bank(b0 + n), hT[hb][:, jj, ti * 128:(ti + 1) * 128], WO[s][:, jj, n * 512:(n + 1) * 512],
                               jj == 0, jj == len(js) - 1, r=[BhT[hb], BWo[s]], w=[Bbank[b0 + n]])
                    STT(X[:, t, :], ps[:, b0 * 512:b0 * 512 + 1024], 0.5, X[:, t, :], ALU.mult, ALU.add,
                        r=[Bbank[b0], Bbank[b0 + 1], BX[t]], w=[BX[t]])
                    if ph == 0 and last and t < 2:
                        pre_brs[(2, t)] = norm_stats(2, t, xnb[0], Bxnb[0])
                    if final and last:
                        col = 3 * NT + t
                        bss = Buf(); brs = Buf()
                        ACT(xnb[0], X[:, t, :], AF.Square, r=[BX[t]], w=[Bxnb[0], bss], accum_out=ssq[:, col:col + 1])
                        rstd_from_ss(ssq[:, col:col + 1], rsd[:, col:col + 1], D, bss, brs)
                        fin_pending.append((t, col, brs))
                        if len(fin_pending) > 2:
                            fin_emit()

            for gi in range(len(groups)):
                prev = None
                for bi in range(len(blocks)):
                    hb = GU(gi, bi)
                    if prev is not None:
                        Y(gi, prev[0], prev[1])
                    prev = (bi, hb)
                Y(gi, prev[0], prev[1])
                if gi + 2 < len(groups):
                    load_group(w_in_d, w_out_d, gi + 2)
                elif gi + 2 == len(groups) and after_slot0 is not None:
                    after_slot0()
            while fin_pending:
                fin_emit()

        tail_chunks = []
        mix_export = {}

        def mixer():
            P.barrier()
            A = Alloc()
            A.off = SLOT1_OFF
            XTm = [A([8, 128], BF16) for _ in range(2)]
            xnb1 = A([D], BF16)
            xnb = [xnb1, xnb1]
            u_, g2_, vb_, vab_, qh_, kt_, kh_, dec_ = [], [], [], [], [], [], [], []
            set_off = []
            for _ in range(2):
                set_off.append(A.off)
                u_.append(A([512], F32)); g2_.append(A([512], F32)); vb_.append(A([512], BF16)); vab_.append(A([512], BF16))
                qh_.append(A([2, 2, 128], BF16))
                kt_.append(A([2, 128], BF16)); kh_.append(A([256], BF16))
                dec_.append(A([2, 128], F32))
            S0b = view(set_off[1], [16, 128], F32)
            Vblk2 = view(set_off[1] + 8192, [4, 128], BF16)
            assert A.off - set_off[1] >= 9216
            gv = A([512], F32)
            l_ = A([256], F32)
            EnbT = A([2, 128], F32)
            Eblb = A([256], F32)
            sc_bf = A([4, 128], BF16)
            ycat = A([D], BF16)
            ycatT = A([8, 128], BF16)
            S = A([2, 128], F32)
            Sbf = [A([2, 128], BF16) for _ in range(2)]
            sq2 = A([512], F32)
            S0 = A([16, 128], F32)
            Vblk = A([4, 128], BF16)
            oT_sb = A([4, 128], F32)
            S0bf = A([16, 128], BF16)
            S0bf2 = view(SLOT1_OFF + 2048, [16, 128], BF16)
            build.mix_arena = A.off
            lhi = A([256], BF16); llo = A([256], BF16)
            def BWINf(c0):
                return BWINc[min(c0 // 512, 4)]

            def WINs(c0, n):
                if c0 < 1536:
                    return WINa[:, :, c0:c0 + n]
                return WINb[:, :, c0 - 1536:c0 - 1536 + n]
            Bxnb1 = Buf()
            BXTm = [Buf(), Buf()]; Bxnb = [Bxnb1, Bxnb1]
            Bu = [Buf(), Buf()]; Bg2 = [Buf(), Buf()]; Bvb = [Buf(), Buf()]; Bvab = [Buf(), Buf()]
            Bqh = [Buf(), Buf()]; Bkt = [Buf(), Buf()]; Bkh = [Buf(), Buf()]; Bdec = [Buf(), Buf()]
            Bgv = Buf(); Bl = Buf(); BEn = Buf(); BEb = Buf(); Bsc = Buf(); Bycat = Buf(); BycatT = Buf()
            BS = Buf("S"); BSbf = [Buf(), Buf()]; Bsq2 = Buf(); BS0 = Buf("S0"); BVblk = Buf(); BoT = Buf(); BS0bf = Buf()
            BS0b = Buf("S0b"); BVblk2 = Buf(); BS0bf2 = Buf("S0bf2")
            junk16 = oT_sb.rearrange("p a b -> p (a b)").bitcast(BF16); Bjunk16 = BoT
            S0s = [S0, S0b]; BS0s = [BS0, BS0b]; Vblks = [Vblk, Vblk2]; BVblks = [BVblk, BVblk2]
            Bsm = Buf("sm"); Bsm_o = Buf("sm_o"); Bsm_a = Buf("sm_a"); Bplrh = Buf("plrh"); Blh = Buf("lhl")
            og = sq2; Bog = Bsq2

            def load_winb():
                DMA("pool", WINs(2560, 16), wi[:, 2560:INC].rearrange("(k p) n -> p k n", p=128), r=[], w=[BWINlr])
                for c0, c1 in [(1536, 2048), (2048, 2560)]:
                    DMA("pool", WINs(c0, c1 - c0), wi[:, c0:c1].rearrange("(k p) n -> p k n", p=128), r=[], w=[BWINf(c0)])

            def load_wout():
                for c0 in (0, 512):
                    DMA("pool", WOUT[:, :, c0:c0 + 512], wo[:, c0:c0 + 512].rearrange("(k p) n -> p k n", p=128), r=[], w=[BWOUT])
            DMA("sp", S0, state[:, 0:2, :, :].rearrange("i h d v -> (h d) i v"), r=[], w=[BS0])
            setup_ws()
            P.op("dve", lambda e: e.memset(S, 0.0), w=[BS])
            P.op("dve", lambda e: e.memset(Sbf[0], 0.0), w=[BSbf[0]])
            for p_ in range(2):
                P.op("dve", lambda e, p_=p_: e.memset(qh_[p_], 0.0), w=[Bqh[p_]])

            fb = {"n": 0}

            Bwarm = Buf("warm")

            def prewarm(func):
                ACT(sm[:, 24:25], ONE_AP, func, r=[Bc], w=[Bwarm])

            def fbank():
                b = fb["n"] % 4
                fb["n"] += 1
                return b

            def front(t):
                p = t % 2
                samp = (t == 16)
                Ut = Usb if samp else Ub
                SLt = SLsb if samp else SLb
                brs_ = pre_brs.get((2, t))
                if brs_ is None:
                    brs_ = norm_stats(2, t, xnb[p], Bxnb[p])
                yield
                norm_scale(2, t, xnb[p], Bxnb[p], brs_)
                yield
                norm_T(2, t, xnb[p], Bxnb[p], None, None, XTm[p], BXTm[p], 2, pre=False)
                xt = XTm[p]
                yield

                def proj_tok(c0, n, pb, off=0):
                    for k in range(8):
                        MM(bank(pb)[:, off:off + n], xt[:, k, :], WINs(c0, n)[:, k, :], k == 0, k == 7, r=[BXTm[p], BWINf(c0)], w=[Bbank[pb]])

                b = 0; proj_tok(0, 512, b)
                ACT(u_[p], bank(b), AF.Gelu_apprx_tanh, r=[Bbank[b]], w=[Bu[p]])
                b = 1; proj_tok(512, 512, b)
                ACT(gv, bank(b), AF.Gelu_apprx_tanh, r=[Bbank[b]], w=[Bgv])
                prewarm(AF.Silu)
                ACT(sq2, gv, AF.Square, r=[Bgv], w=[Bsq2])
                P.op("dve", lambda e: e.tensor_reduce(out=sm[:, 0:4], in_=sq2.rearrange("p (g c) -> p g c", g=4), axis=AX.X, op=ALU.add),
                     r=[Bsq2], w=[Bsm])
                TT("pool", gv, gv, gv_bc[:], ALU.mult, r=[Bgv, Bc], w=[Bgv])
                rstd_from_ss(sm[:, 0:4], sm[:, 4:8], 128, Bsm, Bsm)
                yield
                b = 2; proj_tok(1536, 512, b)
                CP("dve", vb_[p], bank(b), r=[Bbank[b]], w=[Bvb[p]])
                b = 3; proj_tok(2048, 512, b)
                ACT(g2_[p], bank(b), AF.Silu, r=[Bbank[b]], w=[Bg2[p]])
                prewarm(AF.Exp)
                TT("pool", g2_[p], g2_[p], gb_bc[:], ALU.mult, r=[Bg2[p], Bc], w=[Bg2[p]])
                yield
                bk = 0; proj_tok(1280, 256, bk)
                bq = 1
                for m in range(4):
                    for k in range(8):
                        MM(bank(bq)[:, m * 128:(m + 1) * 128], WINs(1024 + m * 128, 128)[:, k, :], xt[:, k, :], k == 0, k == 7,
                           r=[BXTm[p], BWINc[2]], w=[Bbank[bq]])
                qk = bank(bq).rearrange("p (m n) -> p m n", m=4)
                for hl in range(2):
                    hp = slice(64 * hl, 64 * hl + 64)
                    STT(qh_[p][hp, :, hl, :], qk[hp, 0:2, :], 0.125, dec_[p][hp, :, :], ALU.mult, ALU.mult, r=[Bbank[bq], Bdec[p]], w=[Bqh[p]])
                TT("dve", kt_[p], qk[:, 2:4, :], EnbT, ALU.mult, r=[Bbank[bq], BEn], w=[Bkt[p]])
                TT("dve", kh_[p], bank(bk)[:, 0:256], Eblb, ALU.mult, r=[Bbank[bk], BEb], w=[Bkh[p]])
                gv3 = gv.rearrange("p (g c) -> p g c", g=4)
                TT("dve", gv3, gv3, bc(sm[:, 4:8], 2, 128), ALU.mult, r=[Bgv, Bsm], w=[Bgv])
                ACT(vab_[p], gv, AF.Copy, r=[Bgv], w=[Bvab[p]])
                if samp:
                    DMA("sp", cv_out, gv, r=[Bgv], w=[], is_output=True)

            def gate(t):
                p = t % 2
                samp = (t == 16)
                Ut = Usb if samp else Ub
                SLt = SLsb if samp else SLb
                xt = XTm[p]
                bz = 2
                for k in range(8):
                    MM(bank(bz)[0:16, 256:384], WINs(2560, 16)[:, k, :], xt[:, k, :], k == 0, k == 7, r=[BXTm[p], BWINlr], w=[Bbank[bz]])
                CP("dve", plrT[0:16, :], bank(bz)[0:16, 256:384], r=[Bbank[bz]], w=[BplrT])
                CP("dve", plrhi[:], plrT[:], r=[BplrT], w=[Bplrh])
                TT("dve", plrlo[:], plrT[:], plrhi[:], ALU.subtract, r=[BplrT, Bplrh], w=[Bplrh])
                yield
                MM(bank(bz)[:, 0:256], plrhi[:], wa2hi[:], True, False, r=[Bplrh, Bwa2], w=[Bbank[bz]])
                MM(bank(bz)[:, 0:256], plrhi[:], wa2lo[:], False, False, r=[Bplrh, Bwa2], w=[Bbank[bz]])
                MM(bank(bz)[:, 0:256], plrlo[:], wa2hi[:], False, True, r=[Bplrh, Bwa2], w=[Bbank[bz]])
                ACT(l_, bank(bz)[:, 0:256], AF.Exp, r=[Bbank[bz]], w=[Bl], scale=-1.0)
                ACT(l_, l_, AF.Ln, r=[Bl, Bc], w=[Bl], bias=ONE_AP)
                CP("dve", lhi, l_, r=[Bl], w=[Blh])
                TT("dve", llo, l_, lhi, ALU.subtract, r=[Bl, Blh], w=[Blh])
                yield
                bcm = 3
                for pp in range(2):
                    MM(bank(bcm)[:, pp * 128:(pp + 1) * 128], lhi[:, pp * 128:(pp + 1) * 128], Ut[:], True, False, r=[Blh, BU], w=[Bbank[bcm]])
                    MM(bank(bcm)[:, pp * 128:(pp + 1) * 128], llo[:, pp * 128:(pp + 1) * 128], Ut[:], False, True, r=[Blh, BU], w=[Bbank[bcm]])
                MM(bank(bcm)[:, 256:512], SLt[:], lhi, True, False, r=[Blh, BU], w=[Bbank[bcm]])
                MM(bank(bcm)[:, 256:512], SLt[:], llo, False, True, r=[Blh, BU], w=[Bbank[bcm]])
                ACT(dec_[p], bank(bcm)[:, 0:256].rearrange("p (a b) -> p a b", a=2), AF.Exp, r=[Bbank[bcm]], w=[Bdec[p]], scale=-1.0 / 16)
                ACT(EnbT, bank(bcm)[:, 0:256].rearrange("p (a b) -> p a b", a=2), AF.Exp, r=[Bbank[bcm]], w=[BEn], scale=1.0 / 16)
                ACT(Eblb, bank(bcm)[:, 256:512], AF.Exp, r=[Bbank[bcm]], w=[BEb], scale=-1.0 / 16)
                prewarm(AF.Gelu_apprx_tanh)

            def chain(t):
                p = t % 2
                samp = (t == 16)
                Ut = Us if samp else U
                WsTt = WsTs if samp else WsT
                bcolt = bcols if samp else bcol
                for h in range(4):
                    hp = slice(64 * (h % 2), 64 * (h % 2) + 64); pp = h // 2
                    MM(bank(4)[:, h * 128:(h + 1) * 128], kt_[p][:, pp, :], qh_[p][:, pp, h % 2, :], True, True, r=[Bkt[p], Bqh[p]], w=[Bbank[4]])
                TT("dve", sc_bf, bank(4).rearrange("p (h t) -> p h t", h=4), bc(Ut[:], 1, 4), ALU.mult, r=[Bbank[4], BU], w=[Bsc])
                yield
                if not samp:
                    sp_ = t % 2
                    for h in range(4):
                        hp = slice(64 * (h % 2), 64 * (h % 2) + 64); pp = h // 2
                        MM(bank(5)[:, h * 128:(h + 1) * 128], sc_bf[:, h, :], vb_[p][:, h * 128:(h + 1) * 128], True, False,
                           r=[Bsc, Bvb[p]], w=[Bbank[5]])
                        MM(bank(5)[:, h * 128:(h + 1) * 128], qh_[p][:, pp, h % 2, :], Sbf[sp_][:, pp, :], False, True,
                           r=[Bqh[p], BSbf[sp_]], w=[Bbank[5]])
                    for h in range(4):
                        pp = h // 2
                        MM(bank(6)[:, h * 128:(h + 1) * 128], kh_[p][:, pp * 128:(pp + 1) * 128], vb_[p][:, h * 128:(h + 1) * 128], True, True,
                           r=[Bkh[p], Bvb[p]], w=[Bbank[6]])
                    for h in range(4):
                        hp = slice(64 * (h % 2), 64 * (h % 2) + 64); pp = h // 2
                        STT(S[hp, pp, :], S[hp, pp, :], dec_[p][hp, pp, 127:128], bank(6)[hp, h * 128:(h + 1) * 128], ALU.mult, ALU.add,
                            r=[BS, Bdec[p], Bbank[6]], w=[BS])
                    CP("dve", Sbf[1 - sp_], S, r=[BS], w=[BSbf[1 - sp_]])
                    if t == 15:
                        for pp in range(2):
                            DMA("sp", sp_out[2 * pp:2 * pp + 2].rearrange("h d v -> (h d) v"), S[:, pp, :], r=[BS], w=[], is_output=True)
                    ob = 5
                else:
                    par1 = [Bu[1], Bg2[1], Bvb[1], Bvab[1], Bqh[1], Bkt[1], Bkh[1], Bdec[1]]
                    DMA("sp", S0b, state[:, 2:4, :, :].rearrange("i h d v -> (h d) i v"), r=[], w=[BS0b] + par1)
                    for pp in range(2):
                        S0q = (S0bf, S0bf2)[pp]; BS0q = (BS0bf, BS0bf2)[pp]
                        for hl in range(2):
                            h = 2 * pp + hl
                            MM(bank(5)[:, h * 128:(h + 1) * 128], vb_[p][:, h * 128:(h + 1) * 128], sc_bf[:, h, :], True, False,
                               r=[Bvb[p], Bsc], w=[Bbank[5]])
                            for i in range(16):
                                MM(bank(5)[:, h * 128 + 8 * i:h * 128 + 8 * i + 8], S0q[:, i, :], qh_[p][:, pp, hl, 8 * i:8 * i + 8], False, i == 15,
                                   r=[BS0q, Bqh[p]], w=[Bbank[5]])
                    CP("dve", oT_sb, bank(5).rearrange("p (h t) -> p h t", h=4), r=[Bbank[5]], w=[BoT])
                    for h in range(4):
                        TR(bank(4)[:, h * 128:(h + 1) * 128], oT_sb[:, h, :], ident_f[:], r=[BoT, Bident], w=[Bbank[4]])
                    ob = 4
                for g in range(4):
                    MM(bank(7)[:, g * 128:(g + 1) * 128], WsTt[:, g, :], vab_[p][:, g * 128:(g + 1) * 128], True, True, r=[BWs, Bvab[p]], w=[Bbank[7]])
                ACT(sq2, bank(ob), AF.Square, r=[Bbank[ob]], w=[Bsq2])
                P.op("dve", lambda e: e.tensor_reduce(out=sm[:, 8:12], in_=sq2.rearrange("p (g c) -> p g c", g=4), axis=AX.X, op=ALU.add),
                     r=[Bsq2], w=[Bsm_o])
                rstd_from_ss(sm[:, 8:12], sm[:, 12:16], 128, Bsm_o, Bsm_o, pool=not samp)
                TT("dve", og, bank(ob), g2_[p], ALU.mult, r=[Bbank[ob], Bg2[p]], w=[Bog])
                for g in range(4):
                    STT(u_[p][:, g * 128:(g + 1) * 128], bank(7)[:, g * 128:(g + 1) * 128], bcolt[:, g:g + 1], u_[p][:, g * 128:(g + 1) * 128],
                        ALU.add, ALU.mult, r=[Bbank[7], Bc, Bu[p]], w=[Bu[p]])
                ACT(ycat[:, 0:512], u_[p], AF.Square, r=[Bu[p]], w=[Bycat, Bsm_a], accum_out=sm[:, 16:17])
                rstd_from_ss(sm[:, 16:17], sm[:, 17:18], 512, Bsm_a, Bsm_a, pool=not samp)
                TT("dve", ycat[:, 512:1024].rearrange("p (h v) -> p h v", h=4), og.rearrange("p (h v) -> p h v", h=4), bc(sm[:, 12:16], 2, 128),
                   ALU.mult, r=[Bog, Bsm_o], w=[Bycat])
                STT(ycat[:, 0:512], u_[p], sm[:, 17:18], ga_bc[:], ALU.mult, ALU.mult, r=[Bu[p], Bsm_a, Bc], w=[Bycat])
                yield
                tb = 4 if ob == 5 else 5
                for k in range(8):
                    TR(bankb(tb)[:, k * 128:(k + 1) * 128], ycat[:, k * 128:(k + 1) * 128], ident_b[:], r=[Bycat, Bident], w=[Bbank[tb]])
                CP("dve", ycatT, bankb(tb).rearrange("p (k n) -> p k n", k=8), r=[Bbank[tb]], w=[BycatT])
                yield
                for n in range(2):
                    for k in range(8):
                        MM(bank(6 + n), ycatT[:, k, :], WOUT[:, k, n * 512:(n + 1) * 512], k == 0, k == 7, r=[BycatT, BWOUT], w=[Bbank[6 + n]])
                    if n == 0:
                        yield
                TT("dve", X[:, t, :], ps[:, 6 * 512:8 * 512], X[:, t, :], ALU.add, r=[Bbank[6], Bbank[7], BX[t]], w=[BX[t]])
                if t < 16:
                    pre_brs[(1, t)] = norm_stats(1, t, junk16, Bjunk16, pool=True)
                if samp:
                    nv = 0
                    for pp in range(2):
                        for hl in range(2):
                            for c in range(4):
                                vk = nv % 2; nv += 1

                                def pre(pp=pp, hl=hl, c=c, vk=vk):
                                    h = 2 * pp + hl
                                    if hl == 0 and c == 0:
                                        dall = dec_[p][:, pp, 7::8]
                                        TT("dve", S0s[pp], S0s[pp], bc(dall, 2, 128), ALU.mult, r=[BS0s[pp], Bdec[p]], w=[BS0s[pp]])
                                    TT("dve", Vblks[vk], bc(vb_[p][:, h * 128:(h + 1) * 128], 1, 4), bc(blkT[:, 4 * c:4 * c + 4], 2, 128), ALU.mult,
                                       r=[Bvb[p], Bblk], w=[BVblks[vk]])

                                def post(pp=pp, hl=hl, c=c, vk=vk, ub=4 + (nv % 4)):
                                    hp = slice(64 * hl, 64 * hl + 64)
                                    S0p = S0s[pp]; BS0p = BS0s[pp]
                                    MM(bank(ub), kh_[p][:, pp * 128:(pp + 1) * 128], Vblks[vk].rearrange("p a b -> p (a b)"), True, True,
                                       r=[Bkh[p], BVblks[vk]], w=[Bbank[ub]])
                                    TT("dve", S0p[hp, 4 * c:4 * c + 4, :], S0p[hp, 4 * c:4 * c + 4, :],
                                       bank(ub)[hp, :].rearrange("p (a b) -> p a b", a=4), ALU.add, r=[BS0p, Bbank[ub]], w=[BS0p])
                                    if hl == 1 and c == 3:
                                        DMA("sp", ss_out[:, 2 * pp:2 * pp + 2, :, :].rearrange("i h d v -> (h d) i v"), S0p, r=[BS0p], w=[], is_output=True)

                                tail_chunks.append((pre, post))

            fr = [front(t) for t in range(NT)] + [None, None]
            ch = [chain(t) for t in range(NT)]
            mix_export["xnb"] = [xnbF[0], S0bf.rearrange("p a b -> p (a b)")[:, 0:D]]

            def adv(g):
                if g is not None:
                    next(g, None)

            fr = fr + [None]
            ga = [gate(t) for t in range(NT)] + [None, None, None]
            adv(fr[0]); adv(fr[0]); adv(fr[0])
            load_winb()
            adv(fr[1]); adv(fr[1])
            adv(fr[0])
            adv(ga[0])
            adv(fr[1])
            adv(fr[0])
            adv(fr[2]); adv(fr[2])
            adv(ga[0])
            adv(ga[1])
            adv(ga[0])
            adv(fr[0])
            load_wout()
            DMA("pool", S0bf, state[:, 0:2, :, :].rearrange("i h d v -> (h d) i v"), r=[], w=[BS0bf])
            adv(ga[1])
            for t in range(NT):
                adv(ch[t])
                adv(fr[t + 2])
                adv(ga[t + 1])
                adv(ch[t])
                adv(fr[t + 1])
                adv(fr[t + 1])
                adv(ch[t])
                adv(fr[t + 1])
                if t + 1 == NT - 1:
                    DMA("pool", S0bf2, state[:, 2:4, :, :].rearrange("i h d v -> (h d) i v"), r=[], w=[BS0bf2, BXTm[1], Bxnb1])
                    load_group(f2wi, f2wo, 0, extra_w=BWINc[0:3])
                    mix_export["g0_tokens"] = list(last_load)
                adv(ga[t + 2])
                adv(fr[t + 3])
                adv(ch[t])
                adv(fr[t + 3])
                adv(ga[t + 2])
                adv(ch[t])
                if t + 3 == NT - 1:
                    DMA("sp", gbc[:], f2n.partition_broadcast(128), r=[], w=[Bgbc])

        def dump_x():
            for t in range(NT):
                DMA("sp", yout[t], X[:, t, :], r=[BX[t]], w=[], is_output=True)

        if stage == 0:
            dump_x()
        elif stage == 1:
            ffn(0, f1n, f1wi, f1wo, barrier=False)
            dump_x()
        elif stage == 2:
            ffn(0, f1n, f1wi, f1wo, barrier=False)
            try:
                mixer()
            except _Cut:
                pass
            dump_x()
        else:
            ffn(0, f1n, f1wi, f1wo, barrier=False, after_slot0=prefetch_wina, next_gamma=mxn)
            mixer()
            ffn(1, f2n, f2wi, f2wo, final=True, preloaded0=True, tail=tail_chunks, xnb_override=mix_export["xnb"], gbc_preloaded=True)

        run, stats = P.emit(sems, dsems)
        with nc.Block() as block:
            @block.tensor
            def _(e): run("pe", e)

            @block.scalar
            def _(e): run("act", e)

            @block.vector
            def _(e): run("dve", e)

            @block.gpsimd
            def _(e): run("pool", e)

            @block.sync
            def _(e): run("sp", e)
        build.stats = stats
    return nc


_NC = None


STAGE = 3
CUT = 0


class _Cut(Exception):
    pass


def cut(n):
    if CUT == n:
        raise _Cut()


def _get_nc():
    global _NC
    if _NC is None:
        _NC = build(STAGE)
    return _NC


def kernel(x_prompt, x_sample, state_gla, ffn1_norm, ffn1_w_in, ffn1_w_out, mix_norm, w_in,
           a_ws, a_bs, a_vnorm, a_onorm, b_wa2, b_ba, b_onorm, w_out, ffn2_norm, ffn2_w_in,
           ffn2_w_out, final_norm):
    f = lambda a: np.ascontiguousarray(np.asarray(a, dtype=np.float32))
    x_prompt = f(x_prompt); x_sample = f(x_sample); state_gla = f(state_gla)
    shared = {
        "f1n": f(ffn1_norm[0]), "f1wi": f(ffn1_w_in[0]), "f1wo": f(ffn1_w_out[0]),
        "mxn": f(mix_norm[0]), "wi": f(w_in[0]), "aws": f(a_ws[0]), "abs": f(a_bs[0]),
        "avn": f(a_vnorm[0]).reshape(512), "aon": f(a_onorm[0]), "wa2": f(b_wa2[0]), "bba": f(b_ba[0]),
        "bon": f(b_onorm[0]).reshape(512), "wo": f(w_out[0]),
        "f2n": f(ffn2_norm[0]), "f2wi": f(ffn2_w_in[0]), "f2wo": f(ffn2_w_out[0]), "fnn": f(final_norm),
    }
    in_maps = []
    for c in range(NCORES):
        xin = np.concatenate([x_prompt[c].reshape(16, 128, D), x_sample[16 * c:16 * c + 16].reshape(1, 128, D)], axis=0)
        m = dict(shared)
        m["xin"] = np.ascontiguousarray(xin)
        m["state"] = np.ascontiguousarray(state_gla[0, 16 * c:16 * c + 16])
        in_maps.append(m)
    nc = _get_nc()
    res = run_bass_kernel_spmd(nc, in_maps, core_ids=list(range(NCORES)))
    y_prompt = np.empty((8, 2048, D), np.float32)
    y_sample = np.empty((128, 8, D), np.float32)
    s_prompt = np.empty((1, 8, 4, 64, 128), np.float32)
    s_sample = np.empty((1, 128, 4, 64, 128), np.float32)
    v_sample = np.empty((1, 128, 8, 512), np.float32)
    for c in range(NCORES):
        r = res.results[c]
        yo = np.asarray(r["yout"])
        y_prompt[c] = yo[0:16].reshape(2048, D)
        y_sample[16 * c:16 * c + 16] = yo[16].reshape(16, 8, D)
        s_prompt[0, c] = np.asarray(r["sp_out"])
        s_sample[0, 16 * c:16 * c + 16] = np.asarray(r["ss_out"])
        v_sample[0, 16 * c:16 * c + 16] = np.asarray(r["cv_out"]).reshape(16, 8, 512)
    return (y_prompt, y_sample, s_prompt, s_sample, v_sample)
```

```python
import contextlib
import numpy as np
import concourse.bass as bass
import concourse.mybir as mybir
from concourse.bass_utils import run_bass_kernel_spmd

F32 = mybir.dt.float32
BF16 = mybir.dt.bfloat16
AF = mybir.ActivationFunctionType
ALU = mybir.AluOpType
AX = mybir.AxisListType

NCORES = 8
NT = 17
D = 1024
DFF = 2816
NJ = DFF // 128
INC = 2576
EPS = 1e-6
GS = 4
RSTD_POOL = True
ENGS = ["pe", "act", "dve", "pool", "sp"]
NCHAN = 8


class Buf:
    __slots__ = ("name", "lw", "readers")

    def __init__(self, name=""):
        self.name = name
        self.lw = None
        self.readers = {}


class Prog:
    def __init__(self, nc):
        self.nc = nc
        self.ops = {e: [] for e in ENGS}
        self.dma_count = {}
        self.dma_rr = {e: 0 for e in ENGS}
        self.out_tokens = []
        self.bar = set()

    @staticmethod
    def _rkey(t):
        return (t[0], t[1]) if t[0] == "c" else (t[0], t[1], t[2])

    def _track(self, r, w, token):
        raw, other = set(), set()
        for b in r:
            if b.lw is not None:
                raw.add(b.lw)
        for b in w:
            if b.lw is not None:
                if b.lw[0] == "c" and b.lw[1] != "pe":
                    raw.add(b.lw)
                else:
                    other.add(b.lw)
            for t in b.readers.values():
                if t[0] == "c" and t[1] != "pe":
                    raw.add(t)
                else:
                    other.add(t)
        k = self._rkey(token)
        for b in r:
            b.readers[k] = token
        for b in w:
            b.readers = {}
            b.lw = token
        raw.discard(token)
        other.discard(token)
        return raw, other

    def barrier(self):
        self.bar = self.snapshot()

    def snapshot(self):
        bar = set()
        for e in ENGS:
            if self.ops[e]:
                for i in range(len(self.ops[e]) - 1, -1, -1):
                    if self.ops[e][i]["dma"] is None:
                        bar.add(("c", e, i))
                        break
        for (q, c), n in self.dma_count.items():
            bar.add(("d", q, c, 16 * n))
        return bar

    def op(self, eng, fn, r=(), w=(), deps=()):
        idx = len(self.ops[eng])
        token = ("c", eng, idx)
        raw, other = self._track(r, w, token)
        d = set(deps)
        for t in raw:
            if t[0] == "c" and t[1] == eng and eng == "pe":
                continue
            d.add(t)
        for t in other | self.bar:
            if t[0] == "c" and t[1] == eng:
                continue
            d.add(t)
        self.ops[eng].append(dict(fn=fn, deps=d, dma=None))
        return token

    def dma(self, queue, fn, r=(), w=(), deps=(), is_output=False):
        chan = self.dma_rr[queue] % NCHAN
        self.dma_rr[queue] += 1
        key = (queue, chan)
        n = self.dma_count.get(key, 0)
        self.dma_count[key] = n + 1
        token = ("d", queue, chan, 16 * (n + 1))
        raw, other = self._track(r, w, token)
        d = set(deps) | raw | other | self.bar
        d.discard(token)
        if n > 0:
            d.add(("d", queue, chan, 16 * n))
        self.ops[queue].append(dict(fn=fn, deps=d, dma=(chan, 16 * (n + 1))))
        if is_output:
            self.out_tokens.append(token)
        return token

    def emit(self, sems, dma_sems, final_engine="sp"):
        if self.out_tokens:
            self.ops[final_engine].append(dict(fn=None, deps=set(self.out_tokens), dma=None))
        need = {e: set() for e in ENGS}
        for e in ENGS:
            for o in self.ops[e]:
                for t in o["deps"]:
                    if t[0] == "c":
                        need[t[1]].add(t[2])
        val = {e: {} for e in ENGS}
        for e in ENGS:
            c = 0
            for i in range(len(self.ops[e])):
                if i in need[e]:
                    c += 1
                    val[e][i] = c
        stats = {e: [len(self.ops[e]), 0, len(need[e])] for e in ENGS}

        def run(e, eng):
            seen = {}
            for i, o in enumerate(self.ops[e]):
                waits = {}
                for t in o["deps"]:
                    if t[0] == "c":
                        key = ("c", t[1]); v = val[t[1]][t[2]]; s = sems[t[1]]
                    else:
                        key = ("d", t[1], t[2]); v = t[3]; s = dma_sems[(t[1], t[2])]
                    if v > seen.get(key, 0):
                        if key not in waits or waits[key][1] < v:
                            waits[key] = (s, v)
                for key, (s, v) in waits.items():
                    eng.wait_ge(s, v)
                    seen[key] = v
                    stats[e][1] += 1
                if o["fn"] is None:
                    continue
                ins = o["fn"](eng)
                if o["dma"] is not None:
                    chan, v = o["dma"]
                    ins.then_inc(dma_sems[(e, chan)], 16)
                elif i in need[e]:
                    ins.then_inc(sems[e], 1)
        return run, stats


def bc(ap, axis, n):
    lst = [list(x) for x in ap.ap]
    lst.insert(axis, [0, n])
    return bass.AP(ap.tensor, ap.offset, lst)


def build(stage=3):
    nc = bass.Bass("TRN2", target_bir_lowering=False)
    dt_in = lambda name, shape: nc.dram_tensor(name, list(shape), F32, kind="ExternalInput").ap()
    dt_out = lambda name, shape: nc.dram_tensor(name, list(shape), F32, kind="ExternalOutput").ap()
    xin = dt_in("xin", [NT, 128, D])
    state = dt_in("state", [16, 4, 64, 128])
    f1n = dt_in("f1n", [D]); f1wi = dt_in("f1wi", [D, 2 * DFF]); f1wo = dt_in("f1wo", [DFF, D])
    mxn = dt_in("mxn", [D]); wi = dt_in("wi", [D, INC])
    aws = dt_in("aws", [4, 128, 128]); abs_ = dt_in("abs", [4, 128]); avn = dt_in("avn", [512])
    aon = dt_in("aon", [512]); wa2 = dt_in("wa2", [16, 256]); bba = dt_in("bba", [256])
    bon = dt_in("bon", [512]); wo = dt_in("wo", [D, D])
    f2n = dt_in("f2n", [D]); f2wi = dt_in("f2wi", [D, 2 * DFF]); f2wo = dt_in("f2wo", [DFF, D])
    fnn = dt_in("fnn", [D])
    yout = dt_out("yout", [NT, 128, D])
    sp_out = dt_out("sp_out", [4, 64, 128])
    ss_out = dt_out("ss_out", [16, 4, 64, 128])
    cv_out = dt_out("cv_out", [128, 512])

    with contextlib.ExitStack() as st:
        E = st.enter_context
        sb = lambda name, shape, dt: E(nc.sbuf_tensor(name, list(shape), dt))
        X = sb("X", [128, NT, D], F32)
        ARENA_BYTES = 115 * 1024
        arena = sb("arena", [128, ARENA_BYTES // 2], BF16)
        ps = E(nc.psum_tensor("ps", [128, 4096], F32))
        psb = ps[:].bitcast(BF16)
        ident_b = sb("ident_b", [128, 128], BF16)
        ident_f = sb("ident_f", [128, 128], F32)
        U = sb("U", [128, 128], F32); SL = sb("SL", [128, 128], F32)
        Us = sb("Us", [128, 128], F32); SLs = sb("SLs", [128, 128], F32)
        blkT = sb("blkT", [128, 16], F32)
        Ub = sb("Ub", [128, 128], BF16); SLb = sb("SLb", [128, 128], BF16)
        Usb = sb("Usb", [128, 128], BF16); SLsb = sb("SLsb", [128, 128], BF16)
        wa2hi = sb("wa2hi", [17, 256], BF16); wa2lo = sb("wa2lo", [17, 256], BF16)
        plrhi = sb("plrhi", [17, 128], BF16); plrlo = sb("plrlo", [17, 128], BF16)
        WsT = sb("WsT", [128, 4, 128], BF16); WsTs = sb("WsTs", [128, 4, 128], BF16)
        Wst = sb("Wst", [128, 4, 128], F32)
        Wst2 = sb("Wst2", [128, 4, 128], F32)
        bcol = sb("bcol", [128, 4], F32); bcols = sb("bcols", [128, 4], F32)
        gbc = sb("gbc", [128, D], F32)
        gv_bc = sb("gv_bc", [128, 512], F32); ga_bc = sb("ga_bc", [128, 512], F32); gb_bc = sb("gb_bc", [128, 512], F32)
        wa2b = sb("wa2b", [17, 256], F32)
        plrT = sb("plrT", [17, 128], F32)
        ssq = sb("ssq", [128, 4 * NT], F32)
        rsd = sb("rsd", [128, 4 * NT], F32)
        sm = sb("sm", [128, 64], F32)

        sems = {e: E(nc.semaphore("s_" + e)) for e in ENGS}
        dsems = {(q, c): E(nc.semaphore(f"d_{q}_{c}")) for q in ("sp", "pool") for c in range(NCHAN)}
        P = Prog(nc)

        def view(off, shape, dt):
            n = int(np.prod(shape))
            isz = 4 if dt == F32 else 2
            assert off % 4 == 0
            a = off // 2
            b = a + n * isz // 2
            assert b * 2 <= ARENA_BYTES, (off, shape)
            ap = arena[:, a:b]
            if dt == F32:
                ap = ap.bitcast(F32)
            if len(shape) == 2:
                ap = ap.rearrange("p (a b) -> p a b", a=shape[0])
            elif len(shape) == 3:
                ap = ap.rearrange("p (a b c) -> p a b c", a=shape[0], b=shape[1])
            return ap

        class Alloc:
            def __init__(self):
                self.off = 0

            def __call__(self, shape, dt):
                n = int(np.prod(shape)) * (4 if dt == F32 else 2)
                n = (n + 31) // 32 * 32
                v = view(self.off, shape, dt)
                self.off += n
                return v

        bank = lambda b, n=512: ps[:, b * 512:b * 512 + n]
        bankb = lambda b: psb[:, b * 1024:(b + 1) * 1024]
        Bbank = [Buf(f"bank{i}") for i in range(8)]
        BX = [Buf(f"X{t}") for t in range(NT)]

        def ACT(out, in_, func, r, w, **kw):
            return P.op("act", lambda e: e.activation(out=out, in_=in_, func=func, **kw), r=r, w=w)

        def TT(eng, out, in0, in1, op, r, w):
            return P.op(eng, lambda e: e.tensor_tensor(out=out, in0=in0, in1=in1, op=op), r=r, w=w)

        def STT(out, in0, scalar, in1, op0, op1, r, w):
            return P.op("dve", lambda e: e.scalar_tensor_tensor(out=out, in0=in0, scalar=scalar, in1=in1, op0=op0, op1=op1), r=r, w=w)

        def TS(eng, out, in0, s1, op0, r, w, s2=None, op1=None):
            if op1 is None:
                return P.op(eng, lambda e: e.tensor_scalar(out=out, in0=in0, scalar1=s1, scalar2=None, op0=op0), r=r, w=w)
            return P.op(eng, lambda e: e.tensor_scalar(out=out, in0=in0, scalar1=s1, scalar2=s2, op0=op0, op1=op1), r=r, w=w)

        def CP(eng, out, in_, r, w):
            return P.op(eng, lambda e: e.tensor_copy(out=out, in_=in_), r=r, w=w)

        def MM(out, lhsT, rhs, start, stop, r, w):
            return P.op("pe", lambda e: e.matmul(out, lhsT=lhsT, rhs=rhs, start=start, stop=stop), r=r, w=w)

        def TR(out, in_, ident, r, w):
            return P.op("pe", lambda e: e.transpose(out=out, in_=in_, identity=ident), r=r, w=w)

        def DMA(q, out, in_, r, w, is_output=False, slow=False, deps=()):
            if slow:
                return P.dma(q, lambda e: e.dma_start(out=out, in_=in_, allow_slow_non_contiguous=True), r=r, w=w, is_output=is_output, deps=deps)
            return P.dma(q, lambda e: e.dma_start(out=out, in_=in_), r=r, w=w, is_output=is_output, deps=deps)

        def rstd_from_ss(ss_ap, rs_ap, n, bss, brs, pool=True):
            if RSTD_POOL and pool:
                wdt = rs_ap.shape[1]
                P.op("pool", lambda e: e.tensor_scalar(out=rs_ap, in0=ss_ap, scalar1=1.0 / n, scalar2=EPS, op0=ALU.mult, op1=ALU.add),
                     r=[bss], w=[brs])
                P.op("pool", lambda e: e.tensor_tensor(out=rs_ap, in0=rs_ap, in1=neghalf[:, 0:wdt], op=ALU.pow), r=[brs, Bc], w=[brs])
            else:
                ACT(rs_ap, ss_ap, AF.Sqrt, r=[bss, Bc], w=[brs], scale=1.0 / n, bias=EPS_AP)
                P.op("dve", lambda e: e.reciprocal(out=rs_ap, in_=rs_ap), r=[brs], w=[brs])

        Bc = Buf("consts")
        epsc = sb("epsc", [128, 1], F32)
        onec = sb("onec", [128, 1], F32)
        neghalf = sb("neghalf", [128, 4], F32)
        EPS_AP = epsc[:, 0:1]
        ONE_AP = onec[:, 0:1]
        Bident = Buf("ident"); BU = Buf("U"); BWs = Buf("Ws"); Bblk = Buf("blk")
        P.op("pool", lambda e: e.memset(epsc[:], EPS), w=[Bc])
        P.op("pool", lambda e: e.memset(onec[:], 1.0), w=[Bc])
        P.op("pool", lambda e: e.memset(neghalf[:], -0.5), w=[Bc])
        P.op("pool", lambda e: e.memset(ident_f[:], 1.0), w=[Bident])
        P.op("pool", lambda e: e.affine_select(out=ident_f[:], in_=ident_f[:], pattern=[[-1, 128]], compare_op=ALU.is_equal,
                                               fill=0.0, base=0, channel_multiplier=1), r=[Bident], w=[Bident])
        CP("pool", ident_b[:], ident_f[:], r=[Bident], w=[Bident])
        P.op("pool", lambda e: e.memset(U[:], 1.0), w=[BU])
        P.op("pool", lambda e: e.affine_select(out=U[:], in_=U[:], pattern=[[1, 128]], compare_op=ALU.is_ge,
                                               fill=0.0, base=0, channel_multiplier=-1), r=[BU], w=[BU])
        P.op("pool", lambda e: e.memset(SL[:], 1.0), w=[BU])
        P.op("pool", lambda e: e.affine_select(out=SL[:], in_=SL[:], pattern=[[-1, 128]], compare_op=ALU.is_ge,
                                               fill=0.0, base=-1, channel_multiplier=1), r=[BU], w=[BU])
        P.op("pool", lambda e: e.memset(blkT[:], 1.0), w=[Bblk])
        P.op("pool", lambda e: e.affine_select(out=blkT[:], in_=blkT[:], pattern=[[-8, 16]], compare_op=ALU.is_ge,
                                               fill=0.0, base=0, channel_multiplier=1), r=[Bblk], w=[Bblk])
        P.op("pool", lambda e: e.affine_select(out=blkT[:], in_=blkT[:], pattern=[[8, 16]], compare_op=ALU.is_ge,
                                               fill=0.0, base=7, channel_multiplier=-1), r=[Bblk], w=[Bblk])
        blk_v = bc(blkT[:], 2, 8)
        TT("pool", Us[:].rearrange("p (a b) -> p a b", b=8), U[:].rearrange("p (a b) -> p a b", b=8), blk_v, ALU.mult, r=[BU, Bblk], w=[BU])
        TT("pool", SLs[:].rearrange("p (a b) -> p a b", b=8), SL[:].rearrange("p (a b) -> p a b", b=8), blk_v, ALU.mult, r=[BU, Bblk], w=[BU])
        for dst_, src_ in ((Ub, U), (SLb, SL), (Usb, Us), (SLsb, SLs)):
            CP("pool", dst_[:], src_[:], r=[BU], w=[BU])
        for t in range(4):
            DMA("sp", X[:, t, :], xin[t], r=[], w=[BX[t]])

        def load_rest_x(deps):
            tok = None
            for t in range(4, NT):
                tok = DMA("sp", X[:, t, :], xin[t], r=[], w=[BX[t]], deps=deps)
            return tok

        tok0 = P.op("pool", lambda e: e.memset(Wst2[:], 0.0), w=[Buf()])
        Bwst = Buf("Wst"); Bwst2 = [Buf(f"Wst2_{i}") for i in range(16)]; Bwa2b = [Buf("wa2b0"), Buf("wa2b1")]

        def load_consts():
          DMA("sp", Wst[:], aws.rearrange("g t s -> t g s"), r=[], w=[Bwst])
          for i in range(16):
            DMA("sp", Wst2[8 * i:8 * i + 8, :, 8 * i:8 * i + 8], aws[:, 0:8, 0:8].rearrange("g t s -> t g s"), r=[], w=[Bwst2[i]], deps=[tok0])
          DMA("sp", bcol[:], abs_.rearrange("g t -> t g"), r=[], w=[Buf()], slow=True)
          for i in range(16):
            DMA("sp", bcols[8 * i:8 * i + 8, :], abs_[:, 0:8].rearrange("g t -> t g"), r=[], w=[Buf()], slow=True)
          DMA("sp", gv_bc[:], avn.partition_broadcast(128), r=[], w=[Buf()])
          DMA("sp", ga_bc[:], aon.partition_broadcast(128), r=[], w=[Buf()])
          DMA("sp", gb_bc[:], bon.partition_broadcast(128), r=[], w=[Buf()])
          DMA("sp", wa2b[0:16, :], wa2, r=[], w=[Bwa2b[0]])
          DMA("sp", wa2b[16:17, :], bba.rearrange("(o n) -> o n", o=1), r=[], w=[Bwa2b[1]])

        BplrT = Buf("plrT"); Bwa2 = Buf("wa2")
        P.op("pool", lambda e: e.memset(plrT[:], 1.0), w=[BplrT])

        def setup_ws():
            for g in range(4):
                TR(bank(0)[:, g * 128:(g + 1) * 128], Wst[:, g, :], ident_f[:], r=[Bident, Bwst], w=[Bbank[0]])
            TT("dve", WsT[:], bank(0).rearrange("p (g t) -> p g t", g=4), bc(U[:], 1, 4), ALU.mult, r=[Bbank[0], BU], w=[BWs])
            for g in range(4):
                TR(bank(1)[:, g * 128:(g + 1) * 128], Wst2[:, g, :], ident_f[:], r=[Bident] + Bwst2, w=[Bbank[1]])
            TT("dve", WsTs[:], bank(1).rearrange("p (g t) -> p g t", g=4), bc(Us[:], 1, 4), ALU.mult, r=[Bbank[1], BU], w=[BWs])
            CP("dve", wa2hi[:], wa2b[:], r=Bwa2b, w=[Bwa2])
            TT("dve", wa2lo[:], wa2b[:], wa2hi[:], ALU.subtract, r=[Bwa2] + Bwa2b, w=[Bwa2])

        Bgbc = Buf("gbc")

        pre_brs = {}

        def norm_stats(ph, t, xnb_ap, Bxnb, pool=None):
            col = ph * NT + t
            bss = Buf(); brs = Buf()
            ACT(xnb_ap, X[:, t, :], AF.Square, r=[BX[t]], w=[Bxnb, bss], accum_out=ssq[:, col:col + 1])
            rstd_from_ss(ssq[:, col:col + 1], rsd[:, col:col + 1], D, bss, brs, pool=(ph == 2) if pool is None else pool)
            return brs

        def norm_scale(ph, t, xnb_ap, Bxnb, brs):
            col = ph * NT + t
            STT(xnb_ap, X[:, t, :], rsd[:, col:col + 1], gbc[:], ALU.mult, ALU.mult, r=[BX[t], brs, Bgbc], w=[Bxnb])

        def norm_pre(ph, t, xnb_ap, Bxnb, sqj_ap=None, Bsqj=None):
            brs = pre_brs.get((ph, t))
            if brs is None:
                brs = norm_stats(ph, t, xnb_ap, Bxnb)
            norm_scale(ph, t, xnb_ap, Bxnb, brs)

        def norm_T(ph, t, xnb_ap, Bxnb, sqj_ap, Bsqj, dstT, BdstT, pbank, pre=True, cp_act=False):
            if pre:
                norm_pre(ph, t, xnb_ap, Bxnb, sqj_ap, Bsqj)
            for k in range(8):
                TR(bankb(pbank)[:, k * 128:(k + 1) * 128], xnb_ap[:, k * 128:(k + 1) * 128], ident_b[:], r=[Bxnb, Bident], w=[Bbank[pbank]])
            if not cp_act:
                CP("dve", dstT, bankb(pbank).rearrange("p (k n) -> p k n", k=8), r=[Bbank[pbank]], w=[BdstT])
            else:
                ACT(dstT, bankb(pbank).rearrange("p (k n) -> p k n", k=8), AF.Copy, r=[Bbank[pbank]], w=[BdstT])

        FA = Alloc()
        XT = FA([8, NT * 128], BF16)
        SLOT0_OFF = FA.off
        WG = [None, None]; WU = [None, None]; WO = [None, None]
        for s_ in range(2):
            WG[s_] = FA([8, GS * 128], BF16); WU[s_] = FA([8, GS * 128], BF16); WO[s_] = FA([GS, D], BF16)
        SLOT1_OFF = SLOT0_OFF + 3 * 8 * GS * 128 * 2
        hT = [FA([GS, 512], BF16) for _ in range(2)]
        sil = [FA([512], F32) for _ in range(2)]
        xnbF = [FA([D], BF16) for _ in range(2)]
        yfin = [FA([D], F32) for _ in range(2)]
        BWg = [Buf("Wg0"), Buf("Wg1")]; BWu = [Buf("Wu0"), Buf("Wu1")]; BWo = [Buf("Wo0"), Buf("Wo1")]
        groups = [[0, 1, 2, 3], [4, 5, 6, 7], [8, 9, 10, 11], [12, 13, 14, 15], [16, 17], [18, 19, 20, 21]]
        WINa = view(SLOT0_OFF, [8, 1536], BF16)
        WINb = view(0, [8, INC - 1536], BF16)
        WOUT = view(8 * (INC - 1536) * 2, [8, D], BF16)
        assert 8 * (INC - 1536) * 2 + 8 * D * 2 <= SLOT0_OFF and SLOT0_OFF + 8 * 1536 * 2 == SLOT1_OFF
        BWINc = [Buf(f"WIN{i}") for i in range(5)]; BWOUT = Buf("WOUT"); BWINlr = Buf("WINlr")

        last_load = []

        def load_group(w_in_d, w_out_d, gi, deps=(), extra_w=(), wo_after=False):
            js = groups[gi]; n = len(js); s = gi % 2; j0 = js[0]
            ex = list(extra_w)
            tkg = DMA("pool", WG[s][:, :, 0:n * 128], w_in_d[:, j0 * 128:(j0 + n) * 128].rearrange("(k p) n -> p k n", p=128), r=[], w=[BWg[s]] + ex, deps=deps)
            tku = DMA("pool", WU[s][:, :, 0:n * 128], w_in_d[:, DFF + j0 * 128:DFF + (j0 + n) * 128].rearrange("(k p) n -> p k n", p=128), r=[], w=[BWu[s]] + ex)
            tko = DMA("pool", WO[s][:, 0:n, :], w_out_d[j0 * 128:(j0 + n) * 128, :].rearrange("(j p) n -> p j n", p=128), r=[], w=[BWo[s]] + ex,
                      deps=([tku] if wo_after else []))
            last_load[:] = [tkg, tku, tko]
            return tku

        def prefetch_wina():
            for ci, (c0, c1) in enumerate([(0, 512), (512, 1024), (1024, 1536)]):
                DMA("pool", WINa[:, :, c0:c1], wi[:, c0:c1].rearrange("(k p) n -> p k n", p=128), r=[], w=[BWg[0], BWu[0], BWo[0], BWINc[ci]])
            setup_ws()

        def ffn(ph, gam, w_in_d, w_out_d, final=False, barrier=True, preloaded0=False, after_slot0=None, tail=None, xnb_override=None,
                gbc_preloaded=False, next_gamma=None):
            if barrier:
                P.barrier()
                if preloaded0:
                    P.bar -= set(mix_export.get("g0_tokens", []))
            xnb = xnb_override if xnb_override is not None else xnbF
            tail = list(tail) if tail else []
            defer_g1 = bool(tail)
            BXT = [Buf(f"XT{t}") for t in range(NT)]
            BhT = [Buf(), Buf()]; Bsil = [Buf(), Buf()]; Bxnb = [Buf(), Buf()]; Byfin = [Buf(), Buf()]

            if not gbc_preloaded:
                DMA("sp", gbc[:], gam.partition_broadcast(128), r=[], w=[Bgbc])
            tokw = None
            if not preloaded0:
                tokw = load_group(w_in_d, w_out_d, 0, wo_after=True)
            if ph == 0:
                tokx = load_rest_x([tokw])
                load_consts()
                load_group(w_in_d, w_out_d, 1, deps=[tokx])
            elif not defer_g1:
                load_group(w_in_d, w_out_d, 1)
            blocks = [list(range(b * 4, b * 4 + 4)) for b in range(4)] + [[16]]
            tstate = {"post": None, "g1": not defer_g1}
            cnt = {"gu": 0, "y": 0, "hb": 0}
            brs0 = {}
            for t in blocks[0]:
                if (ph, t) in pre_brs:
                    brs0[t] = (None, pre_brs[(ph, t)])
                    continue
                col = ph * NT + t
                bss = Buf(); brs0[t] = Buf()
                ACT(xnb[t % 2], X[:, t, :], AF.Square, r=[BX[t]], w=[Bxnb[t % 2], bss], accum_out=ssq[:, col:col + 1])
                brs0[t] = (bss, brs0[t])
            for t in blocks[0]:
                if brs0[t][0] is None:
                    continue
                col = ph * NT + t
                rstd_from_ss(ssq[:, col:col + 1], rsd[:, col:col + 1], D, brs0[t][0], brs0[t][1], pool=False)
            b0 = blocks[0]
            norm_scale(ph, b0[0], xnb[b0[0] % 2], Bxnb[b0[0] % 2], brs0[b0[0]][1])
            for i, t in enumerate(b0):
                if i + 1 < len(b0):
                    t1 = b0[i + 1]
                    norm_scale(ph, t1, xnb[t1 % 2], Bxnb[t1 % 2], brs0[t1][1])
                norm_T(ph, t, xnb[t % 2], Bxnb[t % 2], None, None, XT[:, :, t * 128:(t + 1) * 128], BXT[t], 4 + t % 4, pre=False, cp_act=True)
            pending = [t for b in blocks[1:] for t in b]

            def GU(gi, bi):
                js = groups[gi]; s = gi % 2; tiles = blocks[bi]; ntok = len(tiles) * 128; t0 = tiles[0] * 128
                hb = cnt["hb"] % 2; cnt["hb"] += 1
                for jj, j in enumerate(js):
                    pg = cnt["gu"] % 2; cnt["gu"] += 1
                    rx = [BXT[t] for t in tiles]
                    tn = pending.pop(0) if (gi == 0 and pending) else None
                    if tn is not None:
                        norm_pre(ph, tn, xnb[tn % 2], Bxnb[tn % 2])
                    tpost = None
                    if gi == 0 and tail:
                        tpre, tpost = tail.pop(0)
                        tpre()
                    for k in range(8):
                        MM(bank(pg)[:, 0:ntok], WG[s][:, k, jj * 128:(jj + 1) * 128], XT[:, k, t0:t0 + ntok], k == 0, k == 7,
                           r=rx + [BWg[s]], w=[Bbank[pg]])
                    for k in range(8):
                        MM(bank(2 + pg)[:, 0:ntok], WU[s][:, k, jj * 128:(jj + 1) * 128], XT[:, k, t0:t0 + ntok], k == 0, k == 7,
                           r=rx + [BWu[s]], w=[Bbank[2 + pg]])
                    ACT(sil[pg][:, 0:ntok], bank(pg)[:, 0:ntok], AF.Silu, r=[Bbank[pg]], w=[Bsil[pg]])
                    TT("dve", hT[hb][:, jj, 0:ntok], sil[pg][:, 0:ntok], bank(2 + pg)[:, 0:ntok], ALU.mult,
                       r=[Bsil[pg], Bbank[2 + pg]], w=[BhT[hb]])
                    if tpost is not None:
                        tpost()
                    if gi == 0 and not tail and not tstate["g1"]:
                        tstate["g1"] = True
                        load_group(w_in_d, w_out_d, 1, deps=list(P.snapshot()))
                    if tn is not None:
                        norm_T(ph, tn, xnb[tn % 2], Bxnb[tn % 2], None, None, XT[:, :, tn * 128:(tn + 1) * 128], BXT[tn], 4 + tn % 4, pre=False)
                        if final and not pending:
                            DMA("sp", gbc[:], fnn.partition_broadcast(128), r=[], w=[Bgbc])
                        if next_gamma is not None and not pending and not tstate.get("ng"):
                            tstate["ng"] = True
                            DMA("sp", gbc[:], next_gamma.partition_broadcast(128), r=[], w=[Bgbc])
                return hb

            fin_pending = []

            def fin_emit():
                t, col, brs = fin_pending.pop(0)
                yb = t % 2
                STT(yfin[yb], X[:, t, :], rsd[:, col:col + 1], gbc[:], ALU.mult, ALU.mult, r=[BX[t], brs, Bgbc], w=[Byfin[yb]])
                DMA("sp", yout[t], yfin[yb], r=[Byfin[yb]], w=[], is_output=True)

            def Y(gi, bi, hb):
                js = groups[gi]; s = gi % 2; tiles = blocks[bi]
                last = (gi == len(groups) - 1)
                for ti, t in enumerate(tiles):
                    py = cnt["y"] % 2; cnt["y"] += 1
                    b0 = 4 + 2 * py
                    for jj in range(len(js)):
                        for n in range(2):
                            MM(bank(b0 + n), hT[hb][:, jj, ti * 128:(ti + 1) * 128], WO[s][:, jj, n * 512:(n + 1) * 512],
                               jj == 0, jj == len(js) - 1, r=[BhT[hb], BWo[s]], w=[Bbank[b0 + n]])
                    STT(X[:, t, :], ps[:, b0 * 512:b0 * 512 + 1024], 0.5, X[:, t, :], ALU.mult, ALU.add,
                        r=[Bbank[b0], Bbank[b0 + 1], BX[t]], w=[BX[t]])
                    if ph == 0 and last and t < 2:
                        pre_brs[(2, t)] = norm_stats(2, t, xnb[0], Bxnb[0])
                    if final and last:
                        col = 3 * NT + t
                        bss = Buf(); brs = Buf()
                        ACT(xnb[0], X[:, t, :], AF.Square, r=[BX[t]], w=[Bxnb[0], bss], accum_out=ssq[:, col:col + 1])
                        rstd_from_ss(ssq[:, col:col + 1], rsd[:, col:col + 1], D, bss, brs)
                        fin_pending.append((t, col, brs))
                        if len(fin_pending) > 2:
                            fin_emit()

            for gi in range(len(groups)):
                prev = None
                for bi in range(len(blocks)):
                    hb = GU(gi, bi)
                    if prev is not None:
                        Y(gi, prev[0], prev[1])
                    prev = (bi, hb)
                Y(gi, prev[0], prev[1])
                if gi + 2 < len(groups):
                    load_group(w_in_d, w_out_d, gi + 2)
                elif gi + 2 == len(groups) and after_slot0 is not None:
                    after_slot0()
            while fin_pending:
                fin_emit()

        tail_chunks = []
        mix_export = {}

        def mixer():
            P.barrier()
            A = Alloc()
            A.off = SLOT1_OFF
            XTm = [A([8, 128], BF16) for _ in range(2)]
            xnb1 = A([D], BF16)
            xnb = [xnb1, xnb1]
            u_, g2_, vb_, vab_, qh_, kt_, kh_, dec_ = [], [], [], [], [], [], [], []
            set_off = []
            for _ in range(2):
                set_off.append(A.off)
                u_.append(A([512], F32)); g2_.append(A([512], F32)); vb_.append(A([512], BF16)); vab_.append(A([512], BF16))
                qh_.append(A([2, 2, 128], BF16))
                kt_.append(A([2, 128], BF16)); kh_.append(A([256], BF16))
                dec_.append(A([2, 128], F32))
            S0b = view(set_off[1], [16, 128], F32)
            Vblk2 = view(set_off[1] + 8192, [4, 128], BF16)
            assert A.off - set_off[1] >= 9216
            gv = A([512], F32)
            l_ = A([256], F32)
            EnbT = A([2, 128], F32)
            Eblb = A([256], F32)
            sc_bf = A([4, 128], BF16)
            ycat = A([D], BF16)
            ycatT = A([8, 128], BF16)
            S = A([2, 128], F32)
            Sbf = [A([2, 128], BF16) for _ in range(2)]
            sq2 = A([512], F32)
            S0 = A([16, 128], F32)
            Vblk = A([4, 128], BF16)
            oT_sb = A([4, 128], F32)
            S0bf = A([16, 128], BF16)
            S0bf2 = view(SLOT1_OFF + 2048, [16, 128], BF16)
            build.mix_arena = A.off
            lhi = A([256], BF16); llo = A([256], BF16)
            def BWINf(c0):
                return BWINc[min(c0 // 512, 4)]

            def WINs(c0, n):
                if c0 < 1536:
                    return WINa[:, :, c0:c0 + n]
                return WINb[:, :, c0 - 1536:c0 - 1536 + n]
            Bxnb1 = Buf()
            BXTm = [Buf(), Buf()]; Bxnb = [Bxnb1, Bxnb1]
            Bu = [Buf(), Buf()]; Bg2 = [Buf(), Buf()]; Bvb = [Buf(), Buf()]; Bvab = [Buf(), Buf()]
            Bqh = [Buf(), Buf()]; Bkt = [Buf(), Buf()]; Bkh = [Buf(), Buf()]; Bdec = [Buf(), Buf()]
            Bgv = Buf(); Bl = Buf(); BEn = Buf(); BEb = Buf(); Bsc = Buf(); Bycat = Buf(); BycatT = Buf()
            BS = Buf("S"); BSbf = [Buf(), Buf()]; Bsq2 = Buf(); BS0 = Buf("S0"); BVblk = Buf(); BoT = Buf(); BS0bf = Buf()
            BS0b = Buf("S0b"); BVblk2 = Buf(); BS0bf2 = Buf("S0bf2")
            junk16 = oT_sb.rearrange("p a b -> p (a b)").bitcast(BF16); Bjunk16 = BoT
            S0s = [S0, S0b]; BS0s = [BS0, BS0b]; Vblks = [Vblk, Vblk2]; BVblks = [BVblk, BVblk2]
            Bsm = Buf("sm"); Bsm_o = Buf("sm_o"); Bsm_a = Buf("sm_a"); Bplrh = Buf("plrh"); Blh = Buf("lhl")
            og = sq2; Bog = Bsq2

            def load_winb():
                DMA("pool", WINs(2560, 16), wi[:, 2560:INC].rearrange("(k p) n -> p k n", p=128), r=[], w=[BWINlr])
                for c0, c1 in [(1536, 2048), (2048, 2560)]:
                    DMA("pool", WINs(c0, c1 - c0), wi[:, c0:c1].rearrange("(k p) n -> p k n", p=128), r=[], w=[BWINf(c0)])

            def load_wout():
                for c0 in (0, 512):
                    DMA("pool", WOUT[:, :, c0:c0 + 512], wo[:, c0:c0 + 512].rearrange("(k p) n -> p k n", p=128), r=[], w=[BWOUT])
            DMA("sp", S0, state[:, 0:2, :, :].rearrange("i h d v -> (h d) i v"), r=[], w=[BS0])
            P.op("dve", lambda e: e.memset(S, 0.0), w=[BS])
            P.op("dve", lambda e: e.memset(Sbf[0], 0.0), w=[BSbf[0]])
            for p_ in range(2):
                P.op("dve", lambda e, p_=p_: e.memset(qh_[p_], 0.0), w=[Bqh[p_]])

            fb = {"n": 0}

            Bwarm = Buf("warm")

            def prewarm(func):
                ACT(sm[:, 24:25], ONE_AP, func, r=[Bc], w=[Bwarm])

            def fbank():
                b = fb["n"] % 4
                fb["n"] += 1
                return b

            def front(t):
                p = t % 2
                samp = (t == 16)
                Ut = Usb if samp else Ub
                SLt = SLsb if samp else SLb
                brs_ = pre_brs.get((2, t))
                if brs_ is None:
                    brs_ = norm_stats(2, t, xnb[p], Bxnb[p])
                yield
                norm_scale(2, t, xnb[p], Bxnb[p], brs_)
                yield
                norm_T(2, t, xnb[p], Bxnb[p], None, None, XTm[p], BXTm[p], 2, pre=False)
                xt = XTm[p]
                yield

                def proj_tok(c0, n, pb, off=0):
                    for k in range(8):
                        MM(bank(pb)[:, off:off + n], xt[:, k, :], WINs(c0, n)[:, k, :], k == 0, k == 7, r=[BXTm[p], BWINf(c0)], w=[Bbank[pb]])

                b = 0; proj_tok(0, 512, b)
                ACT(u_[p], bank(b), AF.Gelu_apprx_tanh, r=[Bbank[b]], w=[Bu[p]])
                b = 1; proj_tok(512, 512, b)
                ACT(gv, bank(b), AF.Gelu_apprx_tanh, r=[Bbank[b]], w=[Bgv])
                prewarm(AF.Silu)
                ACT(sq2, gv, AF.Square, r=[Bgv], w=[Bsq2])
                P.op("dve", lambda e: e.tensor_reduce(out=sm[:, 0:4], in_=sq2.rearrange("p (g c) -> p g c", g=4), axis=AX.X, op=ALU.add),
                     r=[Bsq2], w=[Bsm])
                TT("pool", gv, gv, gv_bc[:], ALU.mult, r=[Bgv, Bc], w=[Bgv])
                rstd_from_ss(sm[:, 0:4], sm[:, 4:8], 128, Bsm, Bsm)
                yield
                b = 2; proj_tok(1536, 512, b)
                CP("dve", vb_[p], bank(b), r=[Bbank[b]], w=[Bvb[p]])
                b = 3; proj_tok(2048, 512, b)
                ACT(g2_[p], bank(b), AF.Silu, r=[Bbank[b]], w=[Bg2[p]])
                prewarm(AF.Exp)
                TT("pool", g2_[p], g2_[p], gb_bc[:], ALU.mult, r=[Bg2[p], Bc], w=[Bg2[p]])
                yield
                bk = 0; proj_tok(1280, 256, bk)
                bq = 1
                for m in range(4):
                    for k in range(8):
                        MM(bank(bq)[:, m * 128:(m + 1) * 128], WINs(1024 + m * 128, 128)[:, k, :], xt[:, k, :], k == 0, k == 7,
                           r=[BXTm[p], BWINc[2]], w=[Bbank[bq]])
                qk = bank(bq).rearrange("p (m n) -> p m n", m=4)
                for hl in range(2):
                    hp = slice(64 * hl, 64 * hl + 64)
                    STT(qh_[p][hp, :, hl, :], qk[hp, 0:2, :], 0.125, dec_[p][hp, :, :], ALU.mult, ALU.mult, r=[Bbank[bq], Bdec[p]], w=[Bqh[p]])
                TT("dve", kt_[p], qk[:, 2:4, :], EnbT, ALU.mult, r=[Bbank[bq], BEn], w=[Bkt[p]])
                TT("dve", kh_[p], bank(bk)[:, 0:256], Eblb, ALU.mult, r=[Bbank[bk], BEb], w=[Bkh[p]])
                gv3 = gv.rearrange("p (g c) -> p g c", g=4)
                TT("dve", gv3, gv3, bc(sm[:, 4:8], 2, 128), ALU.mult, r=[Bgv, Bsm], w=[Bgv])
                ACT(vab_[p], gv, AF.Copy, r=[Bgv], w=[Bvab[p]])
                if samp:
                    DMA("sp", cv_out, gv, r=[Bgv], w=[], is_output=True)

            def gate(t):
                p = t % 2
                samp = (t == 16)
                Ut = Usb if samp else Ub
                SLt = SLsb if samp else SLb
                xt = XTm[p]
                bz = 2
                for k in range(8):
                    MM(bank(bz)[0:16, 256:384], WINs(2560, 16)[:, k, :], xt[:, k, :], k == 0, k == 7, r=[BXTm[p], BWINlr], w=[Bbank[bz]])
                CP("dve", plrT[0:16, :], bank(bz)[0:16, 256:384], r=[Bbank[bz]], w=[BplrT])
                CP("dve", plrhi[:], plrT[:], r=[BplrT], w=[Bplrh])
                TT("dve", plrlo[:], plrT[:], plrhi[:], ALU.subtract, r=[BplrT, Bplrh], w=[Bplrh])
                yield
                MM(bank(bz)[:, 0:256], plrhi[:], wa2hi[:], True, False, r=[Bplrh, Bwa2], w=[Bbank[bz]])
                MM(bank(bz)[:, 0:256], plrhi[:], wa2lo[:], False, False, r=[Bplrh, Bwa2], w=[Bbank[bz]])
                MM(bank(bz)[:, 0:256], plrlo[:], wa2hi[:], False, True, r=[Bplrh, Bwa2], w=[Bbank[bz]])
                ACT(l_, bank(bz)[:, 0:256], AF.Exp, r=[Bbank[bz]], w=[Bl], scale=-1.0)
                ACT(l_, l_, AF.Ln, r=[Bl, Bc], w=[Bl], bias=ONE_AP)
                CP("dve", lhi, l_, r=[Bl], w=[Blh])
                TT("dve", llo, l_, lhi, ALU.subtract, r=[Bl, Blh], w=[Blh])
                yield
                bcm = 3
                for pp in range(2):
                    MM(bank(bcm)[:, pp * 128:(pp + 1) * 128], lhi[:, pp * 128:(pp + 1) * 128], Ut[:], True, False, r=[Blh, BU], w=[Bbank[bcm]])
                    MM(bank(bcm)[:, pp * 128:(pp + 1) * 128], llo[:, pp * 128:(pp + 1) * 128], Ut[:], False, True, r=[Blh, BU], w=[Bbank[bcm]])
                MM(bank(bcm)[:, 256:512], SLt[:], lhi, True, False, r=[Blh, BU], w=[Bbank[bcm]])
                MM(bank(bcm)[:, 256:512], SLt[:], llo, False, True, r=[Blh, BU], w=[Bbank[bcm]])
                ACT(dec_[p], bank(bcm)[:, 0:256].rearrange("p (a b) -> p a b", a=2), AF.Exp, r=[Bbank[bcm]], w=[Bdec[p]], scale=-1.0 / 16)
                ACT(EnbT, bank(bcm)[:, 0:256].rearrange("p (a b) -> p a b", a=2), AF.Exp, r=[Bbank[bcm]], w=[BEn], scale=1.0 / 16)
                ACT(Eblb, bank(bcm)[:, 256:512], AF.Exp, r=[Bbank[bcm]], w=[BEb], scale=-1.0 / 16)
                prewarm(AF.Gelu_apprx_tanh)

            def chain(t):
                p = t % 2
                samp = (t == 16)
                Ut = Us if samp else U
                WsTt = WsTs if samp else WsT
                bcolt = bcols if samp else bcol
                for h in range(4):
                    hp = slice(64 * (h % 2), 64 * (h % 2) + 64); pp = h // 2
                    MM(bank(4)[:, h * 128:(h + 1) * 128], kt_[p][:, pp, :], qh_[p][:, pp, h % 2, :], True, True, r=[Bkt[p], Bqh[p]], w=[Bbank[4]])
                TT("dve", sc_bf, bank(4).rearrange("p (h t) -> p h t", h=4), bc(Ut[:], 1, 4), ALU.mult, r=[Bbank[4], BU], w=[Bsc])
                yield
                if not samp:
                    sp_ = t % 2
                    for h in range(4):
                        hp = slice(64 * (h % 2), 64 * (h % 2) + 64); pp = h // 2
                        MM(bank(5)[:, h * 128:(h + 1) * 128], sc_bf[:, h, :], vb_[p][:, h * 128:(h + 1) * 128], True, False,
                           r=[Bsc, Bvb[p]], w=[Bbank[5]])
                        MM(bank(5)[:, h * 128:(h + 1) * 128], qh_[p][:, pp, h % 2, :], Sbf[sp_][:, pp, :], False, True,
                           r=[Bqh[p], BSbf[sp_]], w=[Bbank[5]])
                    for h in range(4):
                        pp = h // 2
                        MM(bank(6)[:, h * 128:(h + 1) * 128], kh_[p][:, pp * 128:(pp + 1) * 128], vb_[p][:, h * 128:(h + 1) * 128], True, True,
                           r=[Bkh[p], Bvb[p]], w=[Bbank[6]])
                    for h in range(4):
                        hp = slice(64 * (h % 2), 64 * (h % 2) + 64); pp = h // 2
                        STT(S[hp, pp, :], S[hp, pp, :], dec_[p][hp, pp, 127:128], bank(6)[hp, h * 128:(h + 1) * 128], ALU.mult, ALU.add,
                            r=[BS, Bdec[p], Bbank[6]], w=[BS])
                    CP("dve", Sbf[1 - sp_], S, r=[BS], w=[BSbf[1 - sp_]])
                    if t == 15:
                        for pp in range(2):
                            DMA("sp", sp_out[2 * pp:2 * pp + 2].rearrange("h d v -> (h d) v"), S[:, pp, :], r=[BS], w=[], is_output=True)
                    ob = 5
                else:
                    par1 = [Bu[1], Bg2[1], Bvb[1], Bvab[1], Bqh[1], Bkt[1], Bkh[1], Bdec[1]]
                    DMA("sp", S0b, state[:, 2:4, :, :].rearrange("i h d v -> (h d) i v"), r=[], w=[BS0b] + par1)
                    for pp in range(2):
                        S0q = (S0bf, S0bf2)[pp]; BS0q = (BS0bf, BS0bf2)[pp]
                        for hl in range(2):
                            h = 2 * pp + hl
                            MM(bank(5)[:, h * 128:(h + 1) * 128], vb_[p][:, h * 128:(h + 1) * 128], sc_bf[:, h, :], True, False,
                               r=[Bvb[p], Bsc], w=[Bbank[5]])
                            for i in range(16):
                                MM(bank(5)[:, h * 128 + 8 * i:h * 128 + 8 * i + 8], S0q[:, i, :], qh_[p][:, pp, hl, 8 * i:8 * i + 8], False, i == 15,
                                   r=[BS0q, Bqh[p]], w=[Bbank[5]])
                    CP("dve", oT_sb, bank(5).rearrange("p (h t) -> p h t", h=4), r=[Bbank[5]], w=[BoT])
                    for h in range(4):
                        TR(bank(4)[:, h * 128:(h + 1) * 128], oT_sb[:, h, :], ident_f[:], r=[BoT, Bident], w=[Bbank[4]])
                    ob = 4
                for g in range(4):
                    MM(bank(7)[:, g * 128:(g + 1) * 128], WsTt[:, g, :], vab_[p][:, g * 128:(g + 1) * 128], True, True, r=[BWs, Bvab[p]], w=[Bbank[7]])
                ACT(sq2, bank(ob), AF.Square, r=[Bbank[ob]], w=[Bsq2])
                P.op("dve", lambda e: e.tensor_reduce(out=sm[:, 8:12], in_=sq2.rearrange("p (g c) -> p g c", g=4), axis=AX.X, op=ALU.add),
                     r=[Bsq2], w=[Bsm_o])
                rstd_from_ss(sm[:, 8:12], sm[:, 12:16], 128, Bsm_o, Bsm_o, pool=not samp)
                TT("dve", og, bank(ob), g2_[p], ALU.mult, r=[Bbank[ob], Bg2[p]], w=[Bog])
                for g in range(4):
                    STT(u_[p][:, g * 128:(g + 1) * 128], bank(7)[:, g * 128:(g + 1) * 128], bcolt[:, g:g + 1], u_[p][:, g * 128:(g + 1) * 128],
                        ALU.add, ALU.mult, r=[Bbank[7], Bc, Bu[p]], w=[Bu[p]])
                ACT(ycat[:, 0:512], u_[p], AF.Square, r=[Bu[p]], w=[Bycat, Bsm_a], accum_out=sm[:, 16:17])
                rstd_from_ss(sm[:, 16:17], sm[:, 17:18], 512, Bsm_a, Bsm_a, pool=not samp)
                TT("dve", ycat[:, 512:1024].rearrange("p (h v) -> p h v", h=4), og.rearrange("p (h v) -> p h v", h=4), bc(sm[:, 12:16], 2, 128),
                   ALU.mult, r=[Bog, Bsm_o], w=[Bycat])
                STT(ycat[:, 0:512], u_[p], sm[:, 17:18], ga_bc[:], ALU.mult, ALU.mult, r=[Bu[p], Bsm_a, Bc], w=[Bycat])
                yield
                tb = 4 if ob == 5 else 5
                for k in range(8):
                    TR(bankb(tb)[:, k * 128:(k + 1) * 128], ycat[:, k * 128:(k + 1) * 128], ident_b[:], r=[Bycat, Bident], w=[Bbank[tb]])
                CP("dve", ycatT, bankb(tb).rearrange("p (k n) -> p k n", k=8), r=[Bbank[tb]], w=[BycatT])
                yield
                for n in range(2):
                    for k in range(8):
                        MM(bank(6 + n), ycatT[:, k, :], WOUT[:, k, n * 512:(n + 1) * 512], k == 0, k == 7, r=[BycatT, BWOUT], w=[Bbank[6 + n]])
                    if n == 0:
                        yield
                TT("dve", X[:, t, :], ps[:, 6 * 512:8 * 512], X[:, t, :], ALU.add, r=[Bbank[6], Bbank[7], BX[t]], w=[BX[t]])
                if t < 16:
                    pre_brs[(1, t)] = norm_stats(1, t, junk16, Bjunk16, pool=True)
                if samp:
                    nv = 0
                    for pp in range(2):
                        for hl in range(2):
                            for c in range(4):
                                vk = nv % 2; nv += 1

                                def pre(pp=pp, hl=hl, c=c, vk=vk):
                                    h = 2 * pp + hl
                                    if hl == 0 and c == 0:
                                        dall = dec_[p][:, pp, 7::8]
                                        TT("dve", S0s[pp], S0s[pp], bc(dall, 2, 128), ALU.mult, r=[BS0s[pp], Bdec[p]], w=[BS0s[pp]])
                                    TT("dve", Vblks[vk], bc(vb_[p][:, h * 128:(h + 1) * 128], 1, 4), bc(blkT[:, 4 * c:4 * c + 4], 2, 128), ALU.mult,
                                       r=[Bvb[p], Bblk], w=[BVblks[vk]])

                                def post(pp=pp, hl=hl, c=c, vk=vk, ub=4 + (nv % 4)):
                                    hp = slice(64 * hl, 64 * hl + 64)
                                    S0p = S0s[pp]; BS0p = BS0s[pp]
                                    MM(bank(ub), kh_[p][:, pp * 128:(pp + 1) * 128], Vblks[vk].rearrange("p a b -> p (a b)"), True, True,
                                       r=[Bkh[p], BVblks[vk]], w=[Bbank[ub]])
                                    TT("dve", S0p[hp, 4 * c:4 * c + 4, :], S0p[hp, 4 * c:4 * c + 4, :],
                                       bank(ub)[hp, :].rearrange("p (a b) -> p a b", a=4), ALU.add, r=[BS0p, Bbank[ub]], w=[BS0p])
                                    if hl == 1 and c == 3:
                                        DMA("sp", ss_out[:, 2 * pp:2 * pp + 2, :, :].rearrange("i h d v -> (h d) i v"), S0p, r=[BS0p], w=[], is_output=True)

                                tail_chunks.append((pre, post))

            fr = [front(t) for t in range(NT)] + [None, None]
            ch = [chain(t) for t in range(NT)]
            mix_export["xnb"] = [xnbF[0], S0bf.rearrange("p a b -> p (a b)")[:, 0:D]]

            def adv(g):
                if g is not None:
                    next(g, None)

            fr = fr + [None]
            ga = [gate(t) for t in range(NT)] + [None, None, None]
            adv(fr[0]); adv(fr[0]); adv(fr[0])
            load_winb()
            adv(fr[1]); adv(fr[1])
            adv(fr[0])
            adv(ga[0])
            adv(fr[1])
            adv(fr[0])
            adv(fr[2]); adv(fr[2])
            adv(ga[0])
            adv(ga[1])
            adv(ga[0])
            adv(fr[0])
            load_wout()
            DMA("pool", S0bf, state[:, 0:2, :, :].rearrange("i h d v -> (h d) i v"), r=[], w=[BS0bf])
            adv(ga[1])
            for t in range(NT):
                adv(ch[t])
                adv(fr[t + 2])
                adv(ga[t + 1])
                adv(ch[t])
                adv(fr[t + 1])
                adv(fr[t + 1])
                adv(ch[t])
                adv(fr[t + 1])
                if t + 1 == NT - 1:
                    DMA("pool", S0bf2, state[:, 2:4, :, :].rearrange("i h d v -> (h d) i v"), r=[], w=[BS0bf2, BXTm[1], Bxnb1])
                    load_group(f2wi, f2wo, 0, extra_w=BWINc[0:3])
                    mix_export["g0_tokens"] = list(last_load)
                adv(ga[t + 2])
                adv(fr[t + 3])
                adv(ch[t])
                adv(fr[t + 3])
                adv(ga[t + 2])
                adv(ch[t])
                if t + 3 == NT - 1:
                    DMA("sp", gbc[:], f2n.partition_broadcast(128), r=[], w=[Bgbc])

        def dump_x():
            for t in range(NT):
                DMA("sp", yout[t], X[:, t, :], r=[BX[t]], w=[], is_output=True)

        if stage == 0:
            dump_x()
        elif stage == 1:
            ffn(0, f1n, f1wi, f1wo, barrier=False)
            dump_x()
        elif stage == 2:
            ffn(0, f1n, f1wi, f1wo, barrier=False)
            try:
                mixer()
            except _Cut:
                pass
            dump_x()
        else:
            ffn(0, f1n, f1wi, f1wo, barrier=False, after_slot0=prefetch_wina, next_gamma=mxn)
            mixer()
            ffn(1, f2n, f2wi, f2wo, final=True, preloaded0=True, tail=tail_chunks, xnb_override=mix_export["xnb"], gbc_preloaded=True)

        run, stats = P.emit(sems, dsems)
        with nc.Block() as block:
            @block.tensor
            def _(e): run("pe", e)

            @block.scalar
            def _(e): run("act", e)

            @block.vector
            def _(e): run("dve", e)

            @block.gpsimd
            def _(e): run("pool", e)

            @block.sync
            def _(e): run("sp", e)
        build.stats = stats
    return nc


_NC = None


STAGE = 3
CUT = 0


class _Cut(Exception):
    pass


def cut(n):
    if CUT == n:
        raise _Cut()


def _get_nc():
    global _NC
    if _NC is None:
        _NC = build(STAGE)
    return _NC


def kernel(x_prompt, x_sample, state_gla, ffn1_norm, ffn1_w_in, ffn1_w_out, mix_norm, w_in,
           a_ws, a_bs, a_vnorm, a_onorm, b_wa2, b_ba, b_onorm, w_out, ffn2_norm, ffn2_w_in,
           ffn2_w_out, final_norm):
    f = lambda a: np.ascontiguousarray(np.asarray(a, dtype=np.float32))
    x_prompt = f(x_prompt); x_sample = f(x_sample); state_gla = f(state_gla)
    shared = {
        "f1n": f(ffn1_norm[0]), "f1wi": f(ffn1_w_in[0]), "f1wo": f(ffn1_w_out[0]),
        "mxn": f(mix_norm[0]), "wi": f(w_in[0]), "aws": f(a_ws[0]), "abs": f(a_bs[0]),
        "avn": f(a_vnorm[0]).reshape(512), "aon": f(a_onorm[0]), "wa2": f(b_wa2[0]), "bba": f(b_ba[0]),
        "bon": f(b_onorm[0]).reshape(512), "wo": f(w_out[0]),
        "f2n": f(ffn2_norm[0]), "f2wi": f(ffn2_w_in[0]), "f2wo": f(ffn2_w_out[0]), "fnn": f(final_norm),
    }
    in_maps = []
    for c in range(NCORES):
        xin = np.concatenate([x_prompt[c].reshape(16, 128, D), x_sample[16 * c:16 * c + 16].reshape(1, 128, D)], axis=0)
        m = dict(shared)
        m["xin"] = np.ascontiguousarray(xin)
        m["state"] = np.ascontiguousarray(state_gla[0, 16 * c:16 * c + 16])
        in_maps.append(m)
    nc = _get_nc()
    res = run_bass_kernel_spmd(nc, in_maps, core_ids=list(range(NCORES)))
    y_prompt = np.empty((8, 2048, D), np.float32)
    y_sample = np.empty((128, 8, D), np.float32)
    s_prompt = np.empty((1, 8, 4, 64, 128), np.float32)
    s_sample = np.empty((1, 128, 4, 64, 128), np.float32)
    v_sample = np.empty((1, 128, 8, 512), np.float32)
    for c in range(NCORES):
        r = res.results[c]
        yo = np.asarray(r["yout"])
        y_prompt[c] = yo[0:16].reshape(2048, D)
        y_sample[16 * c:16 * c + 16] = yo[16].reshape(16, 8, D)
        s_prompt[0, c] = np.asarray(r["sp_out"])
        s_sample[0, 16 * c:16 * c + 16] = np.asarray(r["ss_out"])
        v_sample[0, 16 * c:16 * c + 16] = np.asarray(r["cv_out"]).reshape(16, 8, 512)
    return (y_prompt, y_sample, s_prompt, s_sample, v_sample)
```
